# Optimizing a Trainium2 kernel written in Bass

```python
import math
import jax, jax.numpy as jnp
from jax import lax
import numpy as np

D_MODEL = 1024
BATCH = 4
SEQ = 8192
DEPTH = 1

PLE_DIM = 256
CONV_WIDTH = 4
RG_WIDTH = D_MODEL // 2
RG_BLOCKS = 8
RG_BLOCK_DIM = RG_WIDTH // RG_BLOCKS
RG_C = 8.0
GDN_HEADS = 4
GDN_DK = 128
GDN_DV = 128
GDN_QK_WIDTH = GDN_HEADS * GDN_DK
GDN_V_WIDTH = GDN_HEADS * GDN_DV
GDN_CHUNK = 64
MIX_WIDTH = RG_WIDTH + GDN_V_WIDTH
IN_COLS = 2 * RG_WIDTH + 2 * GDN_QK_WIDTH + 2 * GDN_V_WIDTH + 2 * GDN_HEADS
D_FF = -(-8 * D_MODEL // (3 * 256)) * 256
EPS = 1e-6

kernel_name = "hybrid_rglru_gdeltanet_block"


def rmsnorm(x, w):
    xf = x.astype(jnp.float32)
    var = jnp.mean(xf * xf, axis=-1, keepdims=True)
    return (xf * lax.rsqrt(var + EPS) * w.astype(jnp.float32)).astype(x.dtype)


def causal_dwconv(u, w):
    k_taps = w.shape[0]
    s = u.shape[1]
    up = jnp.pad(u, ((0, 0), (k_taps - 1, 0), (0, 0)))
    out = up[:, 0:s] * w[0]
    for j in range(1, k_taps):
        out = out + up[:, j:j + s] * w[j]
    return out


def l2norm(t):
    return t * lax.rsqrt(jnp.sum(t * t, axis=-1, keepdims=True) + EPS)


def rglru_group(xa, ga, conv_w, conv_b, wx, bx, wa, ba, lam):
    b, s, _ = xa.shape
    u = causal_dwconv(xa.astype(jnp.float32), conv_w.astype(jnp.float32)) + conv_b.astype(jnp.float32)
    uh = u.reshape(b, s, RG_BLOCKS, RG_BLOCK_DIM)
    gate_x = jax.nn.sigmoid(jnp.einsum('bshi,hij->bshj', uh, wx.astype(jnp.float32)).reshape(b, s, RG_WIDTH) + bx)
    gate_a = jax.nn.sigmoid(jnp.einsum('bshi,hij->bshj', uh, wa.astype(jnp.float32)).reshape(b, s, RG_WIDTH) + ba)
    log_a = -RG_C * gate_a * jax.nn.softplus(-lam.astype(jnp.float32))
    a = jnp.exp(log_a)
    mult = jnp.sqrt(jnp.maximum(-jnp.expm1(2.0 * log_a), 0.0))
    first = (jnp.arange(s) == 0)[None, :, None]
    mult = jnp.where(first, 1.0, mult)
    bt = u * gate_x * mult

    def combine(l, r):
        return (l[0] * r[0], r[0] * l[1] + r[1])

    _, h = lax.associative_scan(combine, (a, bt), axis=1)
    return h * jax.nn.gelu(ga.astype(jnp.float32))


def gated_delta_rule_chunked(q, k, v, g, beta):
    b, s, h, dk = q.shape
    dv = v.shape[-1]
    n = s // GDN_CHUNK

    def chunks(t):
        t = t.reshape((b, n, GDN_CHUNK, h) + t.shape[3:])
        return jnp.moveaxis(t, 3, 1)

    q = chunks(q * (dk ** -0.5))
    k = chunks(k)
    v = chunks(v)
    beta = chunks(beta)
    gc = jnp.cumsum(chunks(g), axis=-1)
    idx = jnp.arange(GDN_CHUNK)
    causal = idx[:, None] >= idx[None, :]
    strict = idx[:, None] > idx[None, :]
    diff = gc[..., :, None] - gc[..., None, :]
    decay = jnp.where(causal, jnp.exp(jnp.where(causal, diff, 0.0)), 0.0)
    kb = k * beta[..., None]
    kk = jnp.einsum('bhnid,bhnjd->bhnij', kb, k) * decay
    a_mat = jnp.where(strict, kk, 0.0) + jnp.eye(GDN_CHUNK, dtype=kk.dtype)
    u = lax.linalg.triangular_solve(a_mat, v * beta[..., None], left_side=True, lower=True, unit_diagonal=True)
    w = lax.linalg.triangular_solve(a_mat, kb * jnp.exp(gc)[..., None], left_side=True, lower=True, unit_diagonal=True)
    qk = jnp.einsum('bhnid,bhnjd->bhnij', q, k) * decay
    q_dec = q * jnp.exp(gc)[..., None]
    k_dec = k * jnp.exp(gc[..., -1:] - gc)[..., None]
    g_tot = jnp.exp(gc[..., -1])

    def step(state, inp):
        w_n, u_n, qd_n, kd_n, qk_n, gt_n = inp
        v_new = u_n - jnp.einsum('bhcd,bhde->bhce', w_n, state)
        o_n = jnp.einsum('bhcd,bhde->bhce', qd_n, state) + jnp.einsum('bhij,bhje->bhie', qk_n, v_new)
        state = state * gt_n[..., None, None] + jnp.einsum('bhcd,bhce->bhde', kd_n, v_new)
        return state, o_n

    xs = (jnp.moveaxis(w, 2, 0), jnp.moveaxis(u, 2, 0), jnp.moveaxis(q_dec, 2, 0),
          jnp.moveaxis(k_dec, 2, 0), jnp.moveaxis(qk, 2, 0), jnp.moveaxis(g_tot, 2, 0))
    state0 = jnp.zeros((b, h, dk, dv), jnp.float32)
    _, o = lax.scan(step, state0, xs)
    return jnp.transpose(o, (1, 0, 3, 2, 4)).reshape(b, s, h, dv)


def gdn_group(q, k, v, z, b_logit, a_logit, conv_w, a_log, dt_bias, norm_w):
    b, s, _ = q.shape
    qkv = jnp.concatenate([q, k, v], axis=-1).astype(jnp.float32)
    qkv = jax.nn.silu(causal_dwconv(qkv, conv_w.astype(jnp.float32)))
    q, k, v = jnp.split(qkv, [GDN_QK_WIDTH, 2 * GDN_QK_WIDTH], axis=-1)
    q = l2norm(q.reshape(b, s, GDN_HEADS, GDN_DK))
    k = l2norm(k.reshape(b, s, GDN_HEADS, GDN_DK))
    v = v.reshape(b, s, GDN_HEADS, GDN_DV)
    beta = jax.nn.sigmoid(b_logit.astype(jnp.float32))
    g = -jnp.exp(a_log.astype(jnp.float32)) * jax.nn.softplus(a_logit.astype(jnp.float32) + dt_bias)
    o = gated_delta_rule_chunked(q, k, v, g, beta)
    var = jnp.mean(o * o, axis=-1, keepdims=True)
    zh = z.astype(jnp.float32).reshape(b, s, GDN_HEADS, GDN_DV)
    o = o * lax.rsqrt(var + EPS) * norm_w.astype(jnp.float32) * jax.nn.silu(zh)
    return o.reshape(b, s, GDN_V_WIDTH)


def setup_inputs(seed: int = 0) -> dict:
    key = jax.random.key(seed)
    ks = jax.random.split(key, 26)
    f32 = jnp.float32
    nrm = lambda k, shape, scale: jax.random.normal(k, shape, f32) * scale
    gain = lambda k, shape: 1.0 + 0.02 * jax.random.normal(k, shape, f32)
    L = DEPTH
    x = jax.random.normal(ks[0], (BATCH, SEQ, D_MODEL), f32)
    p = jax.random.normal(ks[1], (DEPTH, BATCH, SEQ, PLE_DIM), f32)
    a_c = jax.random.uniform(ks[9], (L, RG_WIDTH), f32, 0.9, 0.999)
    sig = a_c ** (1.0 / RG_C)
    rg_lambda = jnp.log(sig) - jnp.log1p(-sig)
    gdn_a_log = jnp.log(jax.random.uniform(ks[11], (L, GDN_HEADS), f32, 1.0, 16.0))
    dt = jnp.exp(jax.random.uniform(ks[12], (L, GDN_HEADS), f32, math.log(1e-3), math.log(1e-1)))
    gdn_dt_bias = dt + jnp.log(-jnp.expm1(-dt))
    return {
        "x": x,
        "p": p,
        "norm_mix_w": gain(ks[2], (L, D_MODEL)),
        "w_in": nrm(ks[3], (L, D_MODEL, IN_COLS), D_MODEL ** -0.5),
        "conv_a_w": nrm(ks[4], (L, CONV_WIDTH, RG_WIDTH), CONV_WIDTH ** -0.5),
        "conv_a_b": nrm(ks[5], (L, RG_WIDTH), 0.01),
        "rg_wx": nrm(ks[6], (L, RG_BLOCKS, RG_BLOCK_DIM, RG_BLOCK_DIM), RG_BLOCK_DIM ** -0.5),
        "rg_bx": nrm(ks[7], (L, RG_WIDTH), 0.01),
        "rg_wa": nrm(ks[8], (L, RG_BLOCKS, RG_BLOCK_DIM, RG_BLOCK_DIM), RG_BLOCK_DIM ** -0.5),
        "rg_ba": nrm(ks[10], (L, RG_WIDTH), 0.01),
        "rg_lambda": rg_lambda,
        "conv_qkv_w": nrm(ks[13], (L, CONV_WIDTH, 2 * GDN_QK_WIDTH + GDN_V_WIDTH), CONV_WIDTH ** -0.5),
        "gdn_a_log": gdn_a_log,
        "gdn_dt_bias": gdn_dt_bias,
        "gdn_norm_w": gain(ks[14], (L, GDN_DV)),
        "w_out": nrm(ks[15], (L, MIX_WIDTH, D_MODEL), MIX_WIDTH ** -0.5),
        "norm_ffn_w": gain(ks[16], (L, D_MODEL)),
        "w_gate": nrm(ks[17], (L, D_MODEL, D_FF), D_MODEL ** -0.5),
        "w_up": nrm(ks[18], (L, D_MODEL, D_FF), D_MODEL ** -0.5),
        "w_down": nrm(ks[19], (L, D_FF, D_MODEL), D_FF ** -0.5),
        "norm_ple_w": gain(ks[20], (L, D_MODEL)),
        "w_ple_gate": nrm(ks[21], (L, D_MODEL, D_MODEL), D_MODEL ** -0.5),
        "b_ple_gate": nrm(ks[22], (L, D_MODEL), 0.01),
        "w_ple_proj": nrm(ks[23], (L, PLE_DIM, D_MODEL), PLE_DIM ** -0.5),
        "norm_final_w": gain(ks[24], (D_MODEL,)),
    }


def reference(x, p, norm_mix_w, w_in, conv_a_w, conv_a_b, rg_wx, rg_bx, rg_wa, rg_ba, rg_lambda,
              conv_qkv_w, gdn_a_log, gdn_dt_bias, gdn_norm_w, w_out, norm_ffn_w, w_gate, w_up, w_down,
              norm_ple_w, w_ple_gate, b_ple_gate, w_ple_proj, norm_final_w):
    splits = [RG_WIDTH, 2 * RG_WIDTH,
              2 * RG_WIDTH + GDN_QK_WIDTH, 2 * RG_WIDTH + 2 * GDN_QK_WIDTH,
              2 * RG_WIDTH + 2 * GDN_QK_WIDTH + GDN_V_WIDTH,
              2 * RG_WIDTH + 2 * GDN_QK_WIDTH + 2 * GDN_V_WIDTH,
              2 * RG_WIDTH + 2 * GDN_QK_WIDTH + 2 * GDN_V_WIDTH + GDN_HEADS]
    for i in range(DEPTH):
        h = rmsnorm(x, norm_mix_w[i])
        proj = jnp.einsum('bsd,dc->bsc', h, w_in[i])
        xa, ga, q, k, v, z, b_logit, a_logit = jnp.split(proj, splits, axis=-1)
        ya = rglru_group(xa, ga, conv_a_w[i], conv_a_b[i], rg_wx[i], rg_bx[i], rg_wa[i], rg_ba[i], rg_lambda[i])
        yb = gdn_group(q, k, v, z, b_logit, a_logit, conv_qkv_w[i], gdn_a_log[i], gdn_dt_bias[i], gdn_norm_w[i])
        y = jnp.concatenate([ya, yb], axis=-1).astype(x.dtype)
        x = x + jnp.einsum('bsm,md->bsd', y, w_out[i])
        h = rmsnorm(x, norm_ffn_w[i])
        ff = jax.nn.silu(jnp.einsum('bsd,df->bsf', h, w_gate[i])) * jnp.einsum('bsd,df->bsf', h, w_up[i])
        x = x + jnp.einsum('bsf,fd->bsd', ff, w_down[i])
        h = rmsnorm(x, norm_ple_w[i])
        gate = jax.nn.sigmoid(jnp.einsum('bsd,de->bse', h, w_ple_gate[i]) + b_ple_gate[i])
        x = x + gate * jnp.einsum('bsp,pd->bsd', p[i], w_ple_proj[i])
    return rmsnorm(x, norm_final_w)
```

```python
import os
import numpy as np
from contextlib import ExitStack
from collections import deque
import concourse.bass as bass
import concourse.mybir as mybir
from concourse.bass_utils import run_bass_kernel_spmd

F32 = mybir.dt.float32
BF16 = mybir.dt.bfloat16
AF = mybir.ActivationFunctionType
ALU = mybir.AluOpType

D = 1024
INC = 3080
DFF = 2816
NF = 22
PLE = 256
TB = 512
EPS = 1e-6
SW = 520
NSLOT = 48
RD = int(os.environ.get('KRD', '10'))
T_WOUT = 0
T_GU = 8
T_DOWN = 8 + 44
T_PG = T_DOWN + 22
T_PP = T_PG + 8
T_IN = T_PP + 2
NTILE = T_IN + 25
C_ID, C_ONE, C_CM, C_SM, C_RM, C_SEL, NCST = 0, 128, 256, 320, 384, 896, 1408
P_NMW, P_NFW, P_NPW, P_CAW, P_CAB, P_BX, P_BA, P_LAM, P_CQW, P_GNW, P_ALOG, P_DTB, P_FL, NPRM = \
    0, 8, 16, 24, 40, 44, 48, 52, 56, 104, 105, 106, 107, 112


class Buf:
    __slots__ = ("w", "r", "excl", "owner")

    def __init__(self, excl=False):
        self.w = None
        self.r = {}
        self.excl = excl
        self.owner = None


CUR_TASK = [None]


class T:
    __slots__ = ("buf", "ap")

    def __init__(self, buf, ap):
        self.buf = buf
        self.ap = ap

    def __getitem__(self, k):
        return T(self.buf, self.ap[k])


class Stream:
    def __init__(self, sem):
        self.sem = sem
        self.count = 0


class Eng:
    def __init__(self, name, skip_self=False):
        self.name = name
        self.stream = None
        self.seen = {}
        self.prog = []
        self.skip_self = skip_self

    def _deps(self, reads, writes):
        need = {}
        for t in reads:
            w = t.buf.w
            if w is not None and need.get(w[0], 0) < w[1]:
                need[w[0]] = w[1]
            if t.buf.excl:
                for s, v in t.buf.r.items():
                    if s is not self.stream and need.get(s, 0) < v:
                        need[s] = v
        for t in writes:
            b = t.buf
            if b.w is not None and need.get(b.w[0], 0) < b.w[1]:
                need[b.w[0]] = b.w[1]
            for s, v in b.r.items():
                if need.get(s, 0) < v:
                    need[s] = v
        for s, v in need.items():
            if s is self.stream and self.skip_self:
                continue
            if self.seen.get(s, 0) < v:
                self.prog.append(("w", s.sem, v))
                self.seen[s] = v

    def op(self, fn, reads, writes):
        self._deps(reads, writes)
        st = self.stream
        st.count += 1
        c = st.count
        self.prog.append(("o", fn, st.sem, 1))
        for t in reads:
            t.buf.r[st] = c
            if t.buf.excl and t.buf.owner is not CUR_TASK[0]:
                raise RuntimeError("PSUM bank read by a task that did not write it last (interleaving bug)")
        for t in writes:
            t.buf.w = (st, c)
            t.buf.r = {}
            t.buf.owner = CUR_TASK[0]

    def dma(self, out, in_, dstream, xr=(), xw=()):
        rs, ws = [in_] + list(xr), [out] + list(xw)
        self._deps(rs, ws)
        dstream.count += 16
        c = dstream.count
        oa, ia = out.ap, in_.ap
        self.prog.append(("o", lambda h: h.dma_start(out=oa, in_=ia), dstream.sem, 16))
        for t in rs:
            t.buf.r[dstream] = c
        for t in ws:
            t.buf.w = (dstream, c)
            t.buf.r = {}

    def wait_stream(self, s):
        if s.count > 0 and self.seen.get(s, 0) < s.count:
            self.prog.append(("w", s.sem, s.count))
            self.seen[s] = s.count

    def replay(self, h):
        for it in self.prog:
            if it[0] == "w":
                h.wait_ge(it[1], it[2])
            else:
                it[1](h).then_inc(it[2], it[3])


def build(npre, nmain, debug=False):
    KSTOP = os.environ.get('KSTOP', '')
    nc = bass.Bass("TRN2", target_bir_lowering=False)
    es = ExitStack()

    def dram(name, shape, dt, kind="ExternalInput"):
        return T(Buf(), nc.dram_tensor(name, shape, dt, kind=kind).ap())

    def sb(name, shape, dt):
        return T(Buf(), es.enter_context(nc.sbuf_tensor(name, shape, dt))[:])

    TPRE, TMAIN = max(npre, 1) * TB, nmain * TB
    xpre = dram("xpre", [TPRE, D], F32)
    xmain = dram("xmain", [TMAIN, D], F32)
    pmain = dram("pmain", [TMAIN, PLE], F32)
    w_in = dram("w_in", [D, INC], F32)
    w_out = dram("w_out", [D, D], F32)
    w_gate = dram("w_gate", [D, DFF], F32)
    w_up = dram("w_up", [D, DFF], F32)
    w_down = dram("w_down", [DFF, D], F32)
    w_pg = dram("w_pg", [D, D], F32)
    w_pp = dram("w_pp", [PLE, D], F32)
    cst_d = dram("cst", [128, NCST], F32)
    prm_d = dram("prm", [128, NPRM], F32)
    rowb_d = dram("rowb", [128, D], F32)
    bpg_d = dram("bpg", [1, D], F32)
    rgw_d = dram("rgw", [128, 8 * 128], F32)
    out_d = dram("out", [TMAIN, D], F32, kind="ExternalOutput")
    scr = dram("scr", [NTILE, 128, 1024], BF16, kind="Internal")

    PE, ACT, DVE, POOL, SP = Eng("pe", True), Eng("act"), Eng("dve"), Eng("pool"), Eng("sp")
    engs = [PE, ACT, DVE, POOL, SP]
    for e in engs:
        e.stream = Stream(es.enter_context(nc.semaphore("s_" + e.name)))
    streams = [e.stream for e in engs[:4]]

    def mkstream(name):
        st = Stream(es.enter_context(nc.semaphore(name)))
        streams.append(st)
        return st

    st_ring = [mkstream("s_rg%d" % i) for i in range(RD)]
    st_x = [mkstream("s_x%d" % i) for i in range(4)]
    st_p = mkstream("s_p")
    st_ot = [mkstream("s_ot%d" % i) for i in range(2)]
    st_c = [mkstream("s_c%d" % i) for i in range(3)]

    def barrier():
        for e in engs:
            for s in streams:
                e.wait_stream(s)

    ring_t = sb("ring", [128, RD, 1024], BF16)
    ring = [T(Buf(), ring_t.ap[:, i, :]) for i in range(RD)]
    xt_sets, xfull_sets = [], []
    for par in range(2):
        xt_t = sb("xt%d" % par, [128, 4, D], F32)
        xt_sets.append([[T(Buf(), xt_t.ap[:, tt, hf * 512:(hf + 1) * 512]) for hf in range(2)] for tt in range(4)])
        xfull_sets.append([xt_t.ap[:, tt, :] for tt in range(4)])
    hT_sets = []
    for par in range(2):
        hT_t = sb("hT%d" % par, [128, 8, TB], BF16)
        hT_sets.append([T(Buf(), hT_t.ap[:, k, :]) for k in range(8)])
    hTA, hTB = hT_sets
    xnb_t = sb("xnb", [128, 2, D], BF16)
    xnb = [T(Buf(), xnb_t.ap[:, i, :]) for i in range(2)]
    junk = sb("junk", [128, D], BF16)
    yT_t = sb("yT", [128, 8, TB], BF16)
    yT = [T(Buf(), yT_t.ap[:, k, :]) for k in range(8)]
    cst = sb("cstsb", [128, NCST], F32)
    prm = sb("prmsb", [128, NPRM], F32)
    der = sb("der", [128, 16], F32)
    rowb = sb("rowbsb", [128, D], F32)
    bpgb = sb("bpgb", [1, D], BF16)
    rgwb = sb("rgwb", [128, 8 * 128], BF16)
    identb = sb("identb", [128, 128], BF16)
    onesb = sb("onesb", [128, 128], BF16)
    S_t = sb("S", [128, 4, 128], F32)
    Sb_t = sb("Sb", [128, 4, 128], BF16)
    Sf = [T(Buf(), S_t.ap[:, h, :]) for h in range(4)]
    Sb = [T(Buf(), Sb_t.ap[:, h, :]) for h in range(4)]
    hist = sb("hist", [128, 16, 3], F32)
    hists = [T(Buf(), hist.ap[:, i, :]) for i in range(16)]
    hst_t = sb("hstate", [128, 4], F32)
    hstate = [T(Buf(), hst_t.ap[:, j:j + 1]) for j in range(4)]
    ssq = sb("ssq", [128, 8], F32)
    cols = sb("cols", [64, 5, 8, 4], F32)
    gtot = sb("gtot", [128, 4, 8], F32)
    small = sb("smallsb", [64, 4, 128], F32)
    smallb = sb("smallb", [64, 8, 128], BF16)
    Rb = [T(Buf(), smallb.ap[:, i, :]) for i in range(4)]
    vnb = [T(Buf(), smallb.ap[:, 4 + i, :]) for i in range(4)]
    tmpf = [T(Buf(), small.ap[:, i, :]) for i in range(4)]
    pb = sb("pbb", [128, 4, PLE], BF16)
    pT_t = sb("pT", [128, 2, TB], BF16)
    pT = [T(Buf(), pT_t.ap[:, k, :]) for k in range(2)]
    ot_t = sb("ot", [128, 2, D], F32)
    ot = [T(Buf(), ot_t.ap[:, i, :]) for i in range(2)]
    rem = nc.sbuf_bytes_remaining
    rem = rem() if callable(rem) else rem
    NSLOT = (rem - 512) // (SW * 4)
    arena = sb("arena", [128, NSLOT * SW], F32)
    print('NSLOT', NSLOT)
    slots = [T(Buf(), arena.ap[:, i * SW:(i + 1) * SW]) for i in range(NSLOT)]
    free = deque(slots)

    def get():
        return free.popleft()

    def put(*ts):
        for t in ts:
            free.append(T(t.buf, arena.ap[:, 0:SW]) if False else t)

    def bf(t, n=2 * SW):
        return T(t.buf, t.ap.bitcast(BF16))

    banks = []
    for i in range(8):
        pt_ = es.enter_context(nc.psum_tensor("pb%d" % i, [128, 512], F32))
        banks.append(T(Buf(excl=True), pt_[:]))

    def bbf(i):
        return T(banks[i].buf, banks[i].ap.bitcast(BF16))

    def cc(col, n=1):
        return cst.ap[:, col:col + n]

    def pcol(col):
        return prm.ap[:, col:col + 1]

    ident = cst.ap[:, C_ID:C_ID + 128]
    onesf = cst.ap[:, C_ONE:C_ONE + 128]

    rr = [0]

    def evac_eng():
        rr[0] ^= 1
        return ACT if rr[0] else DVE

    SP.dma(cst, cst_d, st_c[0])
    SP.dma(prm, prm_d, st_c[1])
    SP.dma(rowb, rowb_d, st_c[2])
    DVE.op(lambda h: h.tensor_copy(out=identb.ap, in_=ident), [cst], [identb])
    DVE.op(lambda h: h.tensor_copy(out=onesb.ap, in_=onesf), [cst], [onesb])
    DVE.op(lambda h: h.memset(S_t.ap, 0.0), [], Sf)
    DVE.op(lambda h: h.memset(Sb_t.ap, 0.0), [], Sb)
    DVE.op(lambda h: h.memset(hist.ap, 0.0), [], hists)
    DVE.op(lambda h: h.memset(hst_t.ap, 0.0), [], hstate)
    ACT.op(lambda h: h.activation(out=der.ap[:, 0:4], in_=prm.ap[:, P_LAM:P_LAM + 4], func=AF.Exp, scale=-1.0), [prm], [der])
    ACT.op(lambda h: h.activation(out=der.ap[:, 0:4], in_=der.ap[:, 0:4], func=AF.Ln, bias=1.0), [der], [der])
    DVE.op(lambda h: h.tensor_scalar(out=der.ap[:, 4:8], in0=der.ap[:, 0:4], scalar1=-16.0, scalar2=None, op0=ALU.mult), [der], [der])
    DVE.op(lambda h: h.tensor_scalar(out=der.ap[:, 0:4], in0=der.ap[:, 0:4], scalar1=-8.0, scalar2=None, op0=ALU.mult), [der], [der])
    ACT.op(lambda h: h.activation(out=der.ap[0:4, 8:9], in_=prm.ap[0:4, P_ALOG:P_ALOG + 1], func=AF.Exp), [prm], [der])
    DVE.op(lambda h: h.tensor_scalar(out=der.ap[0:4, 8:9], in0=der.ap[0:4, 8:9], scalar1=-1.0, scalar2=None, op0=ALU.mult), [der], [der])

    st_cva, st_cvb, st_cvc = mkstream("s_cva"), mkstream("s_cvb"), mkstream("s_cvc")
    scr_in = T(Buf(), scr.ap)
    POOL.dma(rgwb, rgw_d, st_cvc)
    POOL.dma(bpgb, bpg_d, st_cvc)
    for k in range(8):
        view = scr.ap[T_IN:T_IN + 24].rearrange("f p (k c) -> p f k c", k=8)[:, :, k, :]
        POOL.dma(T(scr_in.buf, view), T(w_in.buf, w_in.ap[k * 128:(k + 1) * 128, 0:3072].rearrange("p (f c) -> p f c", c=128)), st_cva)
        POOL.dma(T(scr_in.buf, scr.ap[T_IN + 24].rearrange("p (k c) -> p k c", k=8)[:, k, 0:8]), T(w_in.buf, w_in.ap[k * 128:(k + 1) * 128, 3072:3080]), st_cva)

    cv_jobs = []

    def pair_tiles(wd, K, base):
        for k in range(K):
            for hf in range(2):
                tix = base + hf * (K // 2) + k // 2
                dst = scr.ap[tix].rearrange("p (i d) -> p i d", i=2)[:, k % 2, :]
                cv_jobs.append((T(scr.buf, dst), T(wd.buf, wd.ap[k * 128:(k + 1) * 128, hf * 512:(hf + 1) * 512])))

    pair_tiles(w_out, 8, T_WOUT)
    for k in range(8):
        for gi, wd in enumerate((w_gate, w_up)):
            view = scr.ap[T_GU:T_GU + 44].rearrange("(f two) p (k c) -> two p f k c", two=2, k=8)[gi][:, :, k, :]
            cv_jobs.append((T(scr.buf, view), T(wd.buf, wd.ap[k * 128:(k + 1) * 128, :].rearrange("p (f c) -> p f c", c=128))))
    pair_tiles(w_down, NF, T_DOWN)
    pair_tiles(w_pg, 8, T_PG)
    pair_tiles(w_pp, 2, T_PP)
    cv_total = len(cv_jobs)

    def cv_issue(n):
        for _ in range(min(n, len(cv_jobs))):
            o_, i_ = cv_jobs.pop(0)
            POOL.dma(o_, i_, st_cvb)

    ringn = [0]

    def ring_load(tix):
        s = ring[ringn[0] % RD]
        SP.dma(s, T((scr_in if tix >= T_IN else scr).buf, scr.ap[tix]), st_ring[ringn[0] % RD])
        ringn[0] += 1
        return s

    def rmsnorm_to_hT(wcol0, xt, xfull, hT, nb=(0, 1, 4, 5)):
        for tt in range(4):
            ACT.op(lambda h, tt=tt: h.activation(out=junk.ap, in_=xfull[tt], func=AF.Square, accum_out=ssq.ap[:, tt:tt + 1]),
                   xt[tt], [junk, ssq])
        ACT.op(lambda h: h.activation(out=ssq.ap[:, 4:8], in_=ssq.ap[:, 0:4], func=AF.Sqrt, scale=1.0 / D, bias=EPS), [ssq], [ssq])
        DVE.op(lambda h: h.reciprocal(out=ssq.ap[:, 4:8], in_=ssq.ap[:, 4:8]), [ssq], [ssq])
        if KSTOP == 'nrm1':
            return
        for tt in range(4):
            xb = xnb[tt % 2]
            if tt % 2 == 0:
                ACT.op(lambda h, tt=tt, xb=xb: h.activation(out=xb.ap, in_=xfull[tt], func=AF.Copy, scale=ssq.ap[:, 4 + tt:5 + tt]),
                       xt[tt] + [ssq], [xb])
            else:
                DVE.op(lambda h, tt=tt, xb=xb: h.tensor_scalar(out=xb.ap, in0=xfull[tt], scalar1=ssq.ap[:, 4 + tt:5 + tt], scalar2=None, op0=ALU.mult),
                       xt[tt] + [ssq], [xb])
            for k in range(8):
                pk = bbf(nb[k // 2])
                PE.op(lambda h, k=k, tt=tt, xb=xb, pk=pk: h.transpose(
                    out=pk.ap[:, (k % 2) * 512 + tt * 128:(k % 2) * 512 + (tt + 1) * 128],
                    in_=xb.ap[:, k * 128:(k + 1) * 128], identity=identb.ap), [xb, identb], [pk])
        if KSTOP == 'nrm2':
            return
        for k in range(8):
            pk = bbf(nb[k // 2])
            src = pk.ap[:, (k % 2) * 512:(k % 2 + 1) * 512]
            e = evac_eng() if os.environ.get('KEV', '') == '' else (DVE if os.environ['KEV'] == 'dve' else ACT)
            if e is ACT:
                e.op(lambda h, k=k, src=src: h.activation(out=hT[k].ap, in_=src, func=AF.Identity, scale=pcol(wcol0 + k)), [pk, prm], [hT[k]])
            else:
                e.op(lambda h, k=k, src=src: h.tensor_scalar(out=hT[k].ap, in0=src, scalar1=pcol(wcol0 + k), scalar2=None, op0=ALU.mult),
                     [pk, prm], [hT[k]])

    def run_tasks(tasks, weights=None):
        tasks = list(tasks)
        weights = dict(zip(tasks, weights)) if weights else {}
        while tasks:
            for t_ in list(tasks):
                for _ in range(weights.get(t_, 1)):
                    prev = CUR_TASK[0]
                    CUR_TASK[0] = t_
                    try:
                        next(t_)
                    except StopIteration:
                        tasks.remove(t_)
                        CUR_TASK[0] = prev
                        break
                    CUR_TASK[0] = prev

    def rrobin_sync(groups):
        parked = []
        for grp in groups:
            active = list(grp)
            while active:
                for t_ in list(active):
                    prev = CUR_TASK[0]
                    CUR_TASK[0] = t_
                    try:
                        v = next(t_)
                    except StopIteration:
                        v = None
                        active.remove(t_)
                    CUR_TASK[0] = prev
                    if v == "SYNC":
                        active.remove(t_)
                        parked.append(t_)
                    yield
        yield from rrobin(parked)

    def rrobin(tasks):
        tasks = list(tasks)
        while tasks:
            for t_ in list(tasks):
                prev = CUR_TASK[0]
                CUR_TASK[0] = t_
                try:
                    next(t_)
                except StopIteration:
                    tasks.remove(t_)
                CUR_TASK[0] = prev
                yield

    ipb = [0]

    def inproj_g(c0, m, bank, hT, rot=(0, 1, 4, 5)):
        if bank is None:
            bk = banks[rot[ipb[0] % 4]]
            ipb[0] += 1
        else:
            bk = banks[bank]
        ch = min(c0 // 128, 24)
        off = c0 - ch * 128
        tl = ring_load(T_IN + ch)
        for k in range(8):
            PE.op(lambda h, k=k, bk=bk, tl=tl: h.matmul(bk.ap[0:m, :], lhsT=tl.ap[:, k * 128 + off:k * 128 + off + m], rhs=hT[k].ap,
                                                        start=(k == 0), stop=(k == 7)), [tl, hT[k]], [bk])
        return T(bk.buf, bk.ap[0:m, :])

    def conv(ps, hidx, wbase, bias_ap):
        ext, c = get(), get()
        hs = hists[hidx]
        DVE.op(lambda h: h.tensor_copy(out=ext.ap[:, 0:3], in_=hs.ap), [hs], [ext])
        ACT.op(lambda h: h.activation(out=ext.ap[:, 3:3 + TB], in_=ps.ap, func=AF.Copy), [ps], [ext])
        if bias_ap is None:
            ACT.op(lambda h: h.activation(out=c.ap[:, 0:TB], in_=ps.ap, func=AF.Copy, scale=pcol(wbase + 3)), [ps, prm], [c])
        else:
            ACT.op(lambda h: h.activation(out=c.ap[:, 0:TB], in_=ps.ap, func=AF.Identity, scale=pcol(wbase + 3), bias=bias_ap), [ps, prm], [c])
        DVE.op(lambda h: h.tensor_copy(out=hs.ap, in_=ext.ap[:, TB:TB + 3]), [ext], [hs])
        for k in (2, 1, 0):
            DVE.op(lambda h, k=k: h.scalar_tensor_tensor(out=c.ap[:, 0:TB], in0=ext.ap[:, k:k + TB], scalar=pcol(wbase + k), in1=c.ap[:, 0:TB],
                                                         op0=ALU.mult, op1=ALU.add), [ext, c, prm], [c])
        put(ext)
        return c

    def tokproj_g(src, K, base, accs, epilogue, bias=False, halves=(0, 1)):
        for hf in halves:
            for kp in range(K // 2):
                tl = ring_load(base + hf * (K // 2) + kp)
                for i in range(2):
                    k = 2 * kp + i
                    for tt in range(4):
                        a = banks[accs[tt]]
                        first = (k == 0) and not bias
                        if k == 0 and bias:
                            PE.op(lambda h, a=a, hf=hf: h.matmul(a.ap, lhsT=onesb.ap[0:1, :], rhs=bpgb.ap[0:1, hf * 512:(hf + 1) * 512],
                                                                 start=True, stop=False), [onesb, bpgb], [a])
                        PE.op(lambda h, a=a, k=k, i=i, tt=tt, tl=tl, first=first: h.matmul(
                            a.ap, lhsT=src[k].ap[:, tt * 128:(tt + 1) * 128], rhs=tl.ap[:, i * 512:(i + 1) * 512],
                            start=first, stop=(k == K - 1)), [src[k], tl], [a])
                yield
            for tt in range(4):
                epilogue(hf, tt, banks[accs[tt]])
            yield

    def tokproj(*a_, **k_):
        for _ in tokproj_g(*a_, **k_):
            pass

    def mk_resid(xt):
        def resid_add(hf, tt, a):
            x = xt[tt][hf]
            DVE.op(lambda h: h.tensor_tensor(out=x.ap, in0=a.ap, in1=x.ap, op=ALU.add), [a, x], [x])
        return resid_add

    def load_and_norm(xsrc, blk, xt, xfull, hT, nb):
        r0 = blk * TB
        for tt in range(4):
            SP.dma(T(xt[tt][0].buf, xfull[tt]), T(xsrc.buf, xsrc.ap[r0 + tt * 128:r0 + (tt + 1) * 128, :]), st_x[tt], xw=[xt[tt][1]])
        rmsnorm_to_hT(P_NMW, xt, xfull, hT, nb)

    def mixer(xsrc, blk, main, first_kind, last_pre, xt, xfull, hT, hgroups, halfm=False, prenormed=False, prefetch=None):
        r0 = blk * TB
        resid_add = mk_resid(xt)

        abanks = (2, 3, 6, 7) if halfm else (0, 1, 4, 5)
        hbank = [None]

        def inproj(c0, m=128):
            return inproj_g(c0, m, hbank[0], hT, abanks)

        if not main:
            cv_issue(-(-cv_total // max(npre, 1)))
        if not prenormed:
            load_and_norm(xsrc, blk, xt, xfull, hT, abanks)
        yield

        def rg_task():
            us, gxs, gas, as_ = [], [], [], []
            for j in range(4):
                ps = inproj(j * 128)
                us.append(conv(ps, j, P_CAW + 4 * j, pcol(P_CAB + j)))
            yield
            for j in range(4):
                u = us[j]
                ubs = get()
                ub = bf(ubs)
                POOL.op(lambda h, u=u, ub=ub: h.tensor_copy(out=ub.ap[:, 0:TB], in_=u.ap[:, 0:TB]), [u], [ub])
                gx, ga = get(), get()
                for (dstt, wofs, bcol_) in ((gx, j, P_BX + j), (ga, 4 + j, P_BA + j)):
                    bk = banks[2 + (wofs // 4)]
                    PE.op(lambda h, bk=bk, wofs=wofs, ub=ub: h.matmul(bk.ap, lhsT=rgwb.ap[:, wofs * 128:(wofs + 1) * 128], rhs=ub.ap[:, 0:TB],
                                                                      start=True, stop=True), [rgwb, ub], [bk])
                    ACT.op(lambda h, bk=bk, dstt=dstt, bcol_=bcol_: h.activation(out=dstt.ap[:, 0:TB], in_=bk.ap, func=AF.Sigmoid, bias=pcol(bcol_)),
                           [bk, prm], [dstt])
                put(ubs)
                gxs.append(gx)
                gas.append(ga)
            yield
            for j in range(4):
                a, ga = get(), gas[j]
                ACT.op(lambda h, a=a, ga=ga, j=j: h.activation(out=a.ap[:, 0:TB], in_=ga.ap[:, 0:TB], func=AF.Exp, scale=der.ap[:, j:j + 1]), [ga, der], [a])
                ACT.op(lambda h, ga=ga, j=j: h.activation(out=ga.ap[:, 0:TB], in_=ga.ap[:, 0:TB], func=AF.Exp, scale=der.ap[:, 4 + j:5 + j]), [ga, der], [ga])
                as_.append(a)
            yield
            for j in range(4):
                ga = gas[j]
                ACT.op(lambda h, ga=ga: h.activation(out=ga.ap[:, 0:TB], in_=ga.ap[:, 0:TB], func=AF.Sqrt, scale=-1.0, bias=1.0), [ga], [ga])
                if first_kind == "one":
                    DVE.op(lambda h, ga=ga: h.memset(ga.ap[:, 0:1], 1.0), [], [ga])
                elif first_kind == "flag":
                    DVE.op(lambda h, ga=ga: h.tensor_scalar(out=ga.ap[:, 0:1], in0=ga.ap[:, 0:1], scalar1=pcol(P_FL), scalar2=pcol(P_FL + 1),
                                                            op0=ALU.mult, op1=ALU.add), [ga, prm], [ga])
            yield
            hhs = []
            yield
            for j in range(4):
                u, gx, mu, a = us[j], gxs[j], gas[j], as_[j]
                POOL.op(lambda h, u=u, gx=gx: h.tensor_tensor(out=gx.ap[:, 0:TB], in0=u.ap[:, 0:TB], in1=gx.ap[:, 0:TB], op=ALU.mult), [u, gx], [gx])
                POOL.op(lambda h, mu=mu, gx=gx: h.tensor_tensor(out=gx.ap[:, 0:TB], in0=gx.ap[:, 0:TB], in1=mu.ap[:, 0:TB], op=ALU.mult), [mu, gx], [gx])
                hh = u
                DVE.op(lambda h, hh=hh, a=a, gx=gx, j=j: h.tensor_tensor_scan(out=hh.ap[:, 0:TB], data0=a.ap[:, 0:TB], data1=gx.ap[:, 0:TB],
                                                                              initial=hstate[j].ap, op0=ALU.mult, op1=ALU.add),
                       [a, gx, hstate[j]], [hh])
                DVE.op(lambda h, hh=hh, j=j: h.tensor_copy(out=hstate[j].ap, in_=hh.ap[:, TB - 1:TB]), [hh], [hstate[j]])
                put(a, mu)
                hhs.append(hh)
            yield
            for j in range(4):
                hh, gx = hhs[j], gxs[j]
                if main:
                    ps = inproj(512 + j * 128)
                    ACT.op(lambda h, ps=ps, gx=gx: h.activation(out=gx.ap[:, 0:TB], in_=ps.ap, func=AF.Gelu_apprx_tanh), [ps], [gx])
                    POOL.op(lambda h, hh=hh, gx=gx, j=j: h.tensor_tensor(out=yT[j].ap, in0=hh.ap[:, 0:TB], in1=gx.ap[:, 0:TB], op=ALU.mult), [hh, gx], [yT[j]])
                put(hh, gx)


        G = {}

        def sc_task():
            bp = inproj(3072, 4)
            brow = get()
            ACT.op(lambda h: h.activation(out=brow.ap[0:4, 0:TB], in_=bp.ap, func=AF.Sigmoid), [bp], [brow])
            yield
            apz = inproj(3076, 4)
            grow, gcrow, kdrow = get(), get(), get()
            ACT.op(lambda h: h.activation(out=grow.ap[0:4, 0:TB], in_=apz.ap, func=AF.Exp, bias=prm.ap[0:4, P_DTB:P_DTB + 1]), [apz, prm], [grow])
            ACT.op(lambda h: h.activation(out=grow.ap[0:4, 0:TB], in_=grow.ap[0:4, 0:TB], func=AF.Ln, bias=1.0), [grow], [grow])
            DVE.op(lambda h: h.tensor_scalar(out=grow.ap[0:4, 0:TB], in0=grow.ap[0:4, 0:TB], scalar1=der.ap[0:4, 8:9], scalar2=None, op0=ALU.mult),
                   [grow, der], [grow])
            DVE.op(lambda h: h.tensor_tensor_scan(out=gcrow.ap[0:4, 0:TB], data0=cst.ap[0:4, C_RM:C_RM + TB], data1=grow.ap[0:4, 0:TB],
                                                  initial=0.0, op0=ALU.mult, op1=ALU.add), [cst, grow], [gcrow])
            yield
            g3 = gcrow.ap[0:4, 0:TB].rearrange("p (c j) -> p c j", j=64)
            DVE.op(lambda h: h.tensor_tensor(out=kdrow.ap[0:4, 0:TB].rearrange("p (c j) -> p c j", j=64), in0=g3[:, :, 63:64].broadcast_to([4, 8, 64]),
                                             in1=g3, op=ALU.subtract), [gcrow], [kdrow])
            colp = banks[3]
            cp4 = colp.ap[0:64, 0:96].rearrange("p (a c h) -> p a c h", a=3, c=8)
            for ai, rowt in enumerate((gcrow, brow, kdrow)):
                for c in range(8):
                    PE.op(lambda h, ai=ai, rowt=rowt, c=c: h.transpose(out=cp4[:, ai, c, :], in_=rowt.ap[0:4, c * 64:(c + 1) * 64], identity=ident[0:4, 0:4]),
                          [rowt, cst], [colp])
            ACT.op(lambda h: h.activation(out=cols.ap[:, 0], in_=cp4[:, 0], func=AF.Exp), [colp], [cols])
            ACT.op(lambda h: h.activation(out=cols.ap[:, 1], in_=cp4[:, 2], func=AF.Exp), [colp], [cols])
            DVE.op(lambda h: h.tensor_copy(out=cols.ap[:, 2], in_=cp4[:, 1]), [colp], [cols])
            DVE.op(lambda h: h.tensor_copy(out=cols.ap[:, 4], in_=cp4[:, 0]), [colp], [cols])
            DVE.op(lambda h: h.tensor_tensor(out=cols.ap[:, 3], in0=cols.ap[:, 2], in1=cols.ap[:, 0], op=ALU.mult), [cols], [cols])
            put(grow, kdrow)
            G['brow'], G['gcrow'] = brow, gcrow


        def head(hd):
            brow, gcrow = G['brow'], G['gcrow']

            def hinproj(c0_):
                hbank[0] = myip
                r_ = inproj(c0_)
                hbank[0] = None
                return r_

            odd = hd % 2
            if halfm:
                ba, bb_ = (6, 7) if odd else (2, 3)
                sml = banks[bb_]
                vset = (ba, bb_, ba)
                ssbank = ba
                gcb, bbk = banks[ba], sml
                myip = bb_
            else:
                sml = banks[7] if odd else banks[3]
                vset = (0, 1, 2) if odd else (4, 5, 6)
                ssbank = (2, 6, 3, 7)[hd]
                gcb, bbk = banks[2], sml
                myip = None
            PE.op(lambda h, hd=hd: h.matmul(gcb.ap, lhsT=cst.ap[0:4, C_SEL + hd * 128:C_SEL + (hd + 1) * 128], rhs=gcrow.ap[0:4, 0:TB],
                                            start=True, stop=True), [cst, gcrow], [gcb])
            PE.op(lambda h, hd=hd: h.matmul(bbk.ap[0:64, :], lhsT=cst.ap[0:4, C_SEL + hd * 128:C_SEL + hd * 128 + 64], rhs=brow.ap[0:4, 0:TB],
                                            start=True, stop=True), [cst, brow], [bbk])
            gambc, E, bm = get(), get(), get()
            gcb3 = gcb.ap.rearrange("p (c j) -> p c j", j=64)
            if main:
                ACT.op(lambda h: h.activation(out=gambc.ap[:, 0:TB], in_=gcb.ap, func=AF.Exp), [gcb], [gambc])
            ACT.op(lambda h, hd=hd: h.activation(out=gtot.ap[:, hd, :], in_=gcb3[:, :, 63], func=AF.Exp), [gcb], [gtot])
            E3 = E.ap[0:64, 0:TB].rearrange("p (c j) -> p c j", j=64)
            DVE.op(lambda h, hd=hd: h.scalar_tensor_tensor(out=E3, in0=gcb3[0:64], scalar=0.0,
                                                           in1=cols.ap[:, 4, :, hd].unsqueeze(2).broadcast_to([64, 8, 64]),
                                                           op0=ALU.add, op1=ALU.subtract), [gcb, cols], [E])
            DVE.op(lambda h: h.tensor_scalar(out=E.ap[0:64, 0:TB], in0=E.ap[0:64, 0:TB], scalar1=0.0, scalar2=None, op0=ALU.min), [E], [E])
            ACT.op(lambda h: h.activation(out=E.ap[0:64, 0:TB], in_=E.ap[0:64, 0:TB], func=AF.Exp), [E], [E])
            sm3 = cst.ap[0:64, C_SM:C_SM + 64].unsqueeze(1).broadcast_to([64, 8, 64])
            cm3 = cst.ap[0:64, C_CM:C_CM + 64].unsqueeze(1).broadcast_to([64, 8, 64])
            bm3 = bm.ap[0:64, 0:TB].rearrange("p (c j) -> p c j", j=64)
            DVE.op(lambda h: h.tensor_tensor(out=bm3, in0=bbk.ap[0:64, :].rearrange("p (c j) -> p c j", j=64), in1=sm3, op=ALU.mult), [bbk, cst], [bm])
            POOL.op(lambda h: h.tensor_tensor(out=bm.ap[0:64, 0:TB], in0=bm.ap[0:64, 0:TB], in1=E.ap[0:64, 0:TB], op=ALU.mult), [bm, E], [bm])
            if main:
                POOL.op(lambda h: h.tensor_tensor(out=E3, in0=E3, in1=cm3, op=ALU.mult), [E, cst], [E])
            GT, DTc = bm, E
            yield

            qkn = get()
            qknb = bf(qkn)
            qd = get()
            qdb = bf(qd)
            which = (("q", 8 + hd, 0), ("k", 12 + hd, 512)) if main else (("k", 12 + hd, 512),)
            for (nm, ch, off) in which:
                ps = hinproj(ch * 128)
                c = conv(ps, 4 + (ch - 8), P_CQW + 4 * (ch - 8), None)
                ACT.op(lambda h, c=c: h.activation(out=c.ap[:, 0:TB], in_=c.ap[:, 0:TB], func=AF.Silu), [c], [c])
                sq = get()
                ACT.op(lambda h, c=c, sq=sq: h.activation(out=sq.ap[:, 0:TB], in_=c.ap[:, 0:TB], func=AF.Square), [c], [sq])
                yield
                ssb = banks[ssbank]
                PE.op(lambda h, sq=sq, ssb=ssb: h.matmul(ssb.ap, lhsT=onesf, rhs=sq.ap[:, 0:TB], start=True, stop=True), [cst, sq], [ssb])
                yield
                ACT.op(lambda h, sq=sq, ssb=ssb: h.activation(out=sq.ap[:, 0:TB], in_=ssb.ap, func=AF.Ln, bias=EPS), [ssb], [sq])
                ACT.op(lambda h, sq=sq: h.activation(out=sq.ap[:, 0:TB], in_=sq.ap[:, 0:TB], func=AF.Exp, scale=-0.5), [sq], [sq])
                scl = (128.0 ** -0.5) if nm == "q" else 1.0
                DVE.op(lambda h, c=c, sq=sq, off=off, scl=scl: h.scalar_tensor_tensor(out=qknb.ap[:, off:off + TB], in0=c.ap[:, 0:TB], scalar=scl,
                                                                                   in1=sq.ap[:, 0:TB], op0=ALU.mult, op1=ALU.mult), [c, sq], [qknb])
                put(c, sq)
                yield
            if main:
                POOL.op(lambda h: h.tensor_tensor(out=qdb.ap[:, 0:TB], in0=qknb.ap[:, 0:TB], in1=gambc.ap[:, 0:TB], op=ALU.mult), [qknb, gambc], [qdb])
            elif last_pre:
                ps = hinproj((8 + hd) * 128)
                ACT.op(lambda h, ps=ps, hd=hd: h.activation(out=hists[4 + hd].ap, in_=ps.ap[:, TB - 3:TB], func=AF.Copy), [ps], [hists[4 + hd]])
            put(gambc)
            ps = hinproj((16 + hd) * 128)
            vc = conv(ps, 4 + 8 + hd, P_CQW + 4 * (8 + hd), None)
            ACT.op(lambda h: h.activation(out=vc.ap[:, 0:TB], in_=vc.ap[:, 0:TB], func=AF.Silu), [vc], [vc])
            yield
            vt = [get(), get()]
            for half in range(2):
                vb = banks[vset[half]]
                for ci in range(4):
                    c = half * 4 + ci
                    PE.op(lambda h, vb=vb, ci=ci, c=c: h.transpose(out=vb.ap[0:64, ci * 128:(ci + 1) * 128], in_=vc.ap[:, c * 64:(c + 1) * 64], identity=ident),
                          [vc, cst], [vb])
                DVE.op(lambda h, vb=vb, half=half, hd=hd: h.tensor_tensor(
                    out=vt[half].ap[0:64, 0:512].rearrange("p (c d) -> p c d", d=128), in0=vb.ap[0:64, :].rearrange("p (c d) -> p c d", d=128),
                    in1=cols.ap[:, 2, half * 4:(half + 1) * 4, hd].unsqueeze(2).broadcast_to([64, 4, 128]), op=ALU.mult), [vb, cols], [vt[half]])
            put(vc)
            yield
            kdt = get()
            kdtb = bf(kdt)
            kb6 = bbf(vset[2])
            for c in range(8):
                PE.op(lambda h, c=c: h.transpose(out=kb6.ap[0:64, c * 128:(c + 1) * 128], in_=qknb.ap[:, 512 + c * 64:512 + (c + 1) * 64], identity=identb.ap),
                      [qknb, identb], [kb6])
            DVE.op(lambda h, hd=hd: h.tensor_tensor(out=kdtb.ap[0:64, 0:1024].rearrange("p (c d) -> p c d", d=128),
                                                    in0=kb6.ap[0:64, :].rearrange("p (c d) -> p c d", d=128),
                                                    in1=cols.ap[:, 1, :, hd].unsqueeze(2).broadcast_to([64, 8, 128]), op=ALU.mult), [kb6, cols], [kdtb])
            yield
            kkb, qkb = banks[vset[0]], banks[vset[1]]
            for c in range(8):
                kc = qknb.ap[:, 512 + c * 64:512 + (c + 1) * 64]
                PE.op(lambda h, c=c, kc=kc: h.matmul(kkb.ap[0:64, c * 64:(c + 1) * 64], lhsT=kc, rhs=kc, start=True, stop=True), [qknb], [kkb])
            if main:
                for c in range(8):
                    kc = qknb.ap[:, 512 + c * 64:512 + (c + 1) * 64]
                    PE.op(lambda h, c=c, kc=kc: h.matmul(qkb.ap[0:64, c * 64:(c + 1) * 64], lhsT=kc, rhs=qknb.ap[:, c * 64:(c + 1) * 64],
                                                         start=True, stop=True), [qknb], [qkb])
            sv1, sv2 = get(), get()
            s1, s2 = bf(sv1), bf(sv2)
            Nn, NTt, IpNT, Pp = s1.ap[0:64, 0:512], s1.ap[0:64, 512:1024], s2.ap[0:64, 0:512], s2.ap[0:64, 512:1024]
            QKT = qdb.ap[0:64, 512:1024]
            id3 = cst.ap[0:64, C_ID:C_ID + 64].unsqueeze(1).broadcast_to([64, 8, 64])
            r3 = lambda ap_: ap_.rearrange("p (c j) -> p c j", j=64)
            DVE.op(lambda h: h.scalar_tensor_tensor(out=Nn, in0=kkb.ap[0:64, :], scalar=-1.0, in1=GT.ap[0:64, 0:TB], op0=ALU.mult, op1=ALU.mult),
                   [kkb, GT], [s1])
            if main:
                DVE.op(lambda h: h.scalar_tensor_tensor(out=QKT, in0=qkb.ap[0:64, :], scalar=-1.0, in1=DTc.ap[0:64, 0:TB], op0=ALU.mult, op1=ALU.mult), [qkb, DTc], [qdb])
            put(GT, DTc)
            yield
            POOL.op(lambda h: h.tensor_tensor(out=r3(Pp), in0=r3(Nn), in1=id3, op=ALU.add), [s1, cst], [s2])
            ntb = bbf(vset[2])
            for c in range(8):
                PE.op(lambda h, c=c: h.transpose(out=ntb.ap[0:64, c * 64:(c + 1) * 64], in_=Nn[:, c * 64:(c + 1) * 64], identity=identb.ap[0:64, 0:64]),
                      [s1, identb], [ntb])
            ACT.op(lambda h: h.activation(out=NTt, in_=ntb.ap[0:64, 0:512], func=AF.Copy), [ntb], [s1])
            yield
            pN, pNT, pP = banks[vset[0]], banks[vset[1]], banks[vset[2]]
            for lvl in range(1, 6):
                for c in range(8):
                    sl = slice(c * 64, (c + 1) * 64)
                    PE.op(lambda h, sl=sl: h.matmul(pNT.ap[0:64, sl], lhsT=Nn[:, sl], rhs=NTt[:, sl], start=True, stop=True), [s1], [pNT])
                if lvl < 5:
                    for c in range(8):
                        sl = slice(c * 64, (c + 1) * 64)
                        PE.op(lambda h, sl=sl: h.matmul(pN.ap[0:64, sl], lhsT=NTt[:, sl], rhs=Nn[:, sl], start=True, stop=True), [s1], [pN])
                DVE.op(lambda h: h.tensor_tensor(out=r3(IpNT), in0=r3(pNT.ap[0:64, :]), in1=id3, op=ALU.add), [pNT, cst], [s2])
                if lvl < 5:
                    ACT.op(lambda h: h.activation(out=NTt, in_=pNT.ap[0:64, :], func=AF.Copy), [pNT], [s1])
                    ACT.op(lambda h: h.activation(out=Nn, in_=pN.ap[0:64, :], func=AF.Copy), [pN], [s1])
                yield
                for c in range(8):
                    sl = slice(c * 64, (c + 1) * 64)
                    PE.op(lambda h, sl=sl: h.matmul(pP.ap[0:64, sl], lhsT=IpNT[:, sl], rhs=Pp[:, sl], start=True, stop=True), [s2], [pP])
                DVE.op(lambda h: h.tensor_copy(out=Pp, in_=pP.ap[0:64, :]), [pP], [s2])
                yield
            put(sv1)
            yield "SYNC"
            if prefetch is not None and hd == hgroups[-1][0]:
                prefetch()
            if os.environ.get('KDBG'):
                print('free slots at chunk loop', len(free), 'hd', hd, 'halfm', halfm)
            osb = get() if main else None
            for c in range(8):
                sl = slice(c * 64, (c + 1) * 64)
                kc = qknb.ap[:, 512 + c * 64:512 + (c + 1) * 64]
                i2 = hd
                PE.op(lambda h, kc=kc, hd=hd: h.matmul(sml.ap[0:64, 0:128], lhsT=kc, rhs=Sb[hd].ap, start=True, stop=True), [qknb, Sb[hd]], [sml])
                vsrc = vt[c // 4]
                DVE.op(lambda h, c=c, hd=hd, i2=i2, vsrc=vsrc: h.scalar_tensor_tensor(
                    out=Rb[i2].ap, in0=sml.ap[0:64, 0:128], scalar=cols.ap[:, 3, c, hd:hd + 1], in1=vsrc.ap[0:64, (c % 4) * 128:(c % 4 + 1) * 128],
                    op0=ALU.mult, op1=ALU.subtract), [sml, cols, vsrc], [Rb[i2]])
                yield
                PE.op(lambda h, sl=sl, i2=i2: h.matmul(sml.ap[0:64, 128:256], lhsT=Pp[:, sl], rhs=Rb[i2].ap, start=True, stop=True), [s2, Rb[i2]], [sml])
                ACT.op(lambda h, i2=i2: h.activation(out=vnb[i2].ap, in_=sml.ap[0:64, 128:256], func=AF.Copy), [sml], [vnb[i2]])
                yield
                if main:
                    PE.op(lambda h, sl=sl, hd=hd: h.matmul(sml.ap[:, 384:448], lhsT=Sb[hd].ap, rhs=qdb.ap[:, sl], start=True, stop=False), [Sb[hd], qdb], [sml])
                    PE.op(lambda h, sl=sl, i2=i2: h.matmul(sml.ap[:, 384:448], lhsT=vnb[i2].ap, rhs=QKT[:, sl], start=False, stop=True), [vnb[i2], qdb], [sml])
                    ACT.op(lambda h, sl=sl: h.activation(out=osb.ap[:, sl], in_=sml.ap[:, 384:448], func=AF.Copy), [sml], [osb])
                PE.op(lambda h, c=c, i2=i2: h.matmul(sml.ap[:, 256:384], lhsT=kdtb.ap[0:64, c * 128:(c + 1) * 128], rhs=vnb[i2].ap, start=True, stop=True),
                      [kdtb, vnb[i2]], [sml])
                DVE.op(lambda h, c=c, hd=hd: h.scalar_tensor_tensor(out=Sf[hd].ap, in0=Sf[hd].ap, scalar=gtot.ap[:, hd, c:c + 1], in1=sml.ap[:, 256:384],
                                                                  op0=ALU.mult, op1=ALU.subtract), [Sf[hd], gtot, sml], [Sf[hd]])
                ACT.op(lambda h, hd=hd: h.activation(out=Sb[hd].ap, in_=Sf[hd].ap, func=AF.Copy), [Sf[hd]], [Sb[hd]])
                yield
            put(vt[0], vt[1], kdt, sv2, qkn)
            if main:
                osq, rs = get(), get()
                ACT.op(lambda h: h.activation(out=osq.ap[:, 0:TB], in_=osb.ap[:, 0:TB], func=AF.Square), [osb], [osq])
                yield
                ssb = banks[ssbank]
                PE.op(lambda h: h.matmul(ssb.ap, lhsT=onesf, rhs=osq.ap[:, 0:TB], start=True, stop=True), [cst, osq], [ssb])
                if not halfm:
                    yield
                ACT.op(lambda h: h.activation(out=rs.ap[:, 0:TB], in_=ssb.ap, func=AF.Ln, scale=1.0 / 128, bias=EPS), [ssb], [rs])
                ACT.op(lambda h: h.activation(out=rs.ap[:, 0:TB], in_=rs.ap[:, 0:TB], func=AF.Exp, scale=-0.5), [rs], [rs])
                DVE.op(lambda h: h.scalar_tensor_tensor(out=osq.ap[:, 0:TB], in0=osb.ap[:, 0:TB], scalar=pcol(P_GNW), in1=rs.ap[:, 0:TB], op0=ALU.mult, op1=ALU.mult),
                       [osb, prm, rs], [osq])
                ps = hinproj((20 + hd) * 128)
                ACT.op(lambda h, ps=ps: h.activation(out=rs.ap[:, 0:TB], in_=ps.ap, func=AF.Silu), [ps], [rs])
                POOL.op(lambda h, hd=hd: h.tensor_tensor(out=yT[4 + hd].ap, in0=osq.ap[:, 0:TB], in1=rs.ap[:, 0:TB], op=ALU.mult), [osq, rs], [yT[4 + hd]])
                put(osq, rs, osb)
            put(qd)
            yield
        yield from rrobin([rg_task(), sc_task()])
        if halfm:
            yield from rrobin_sync([[head(i_) for i_ in grp] for grp in hgroups])
        else:
            for grp in hgroups:
                yield from rrobin([head(i_) for i_ in grp])
        put(G['brow'], G['gcrow'])
        if not main:
            return
        tokproj(yT, 8, T_WOUT, [2, 3, 6, 7] if halfm else [4, 5, 6, 7], resid_add)
        yield

    def zpart(blk, xt, xfull, hT):
        r0 = blk * TB
        resid_add = mk_resid(xt)
        POOL.dma(pb, T(pmain.buf, pmain.ap[r0:r0 + TB, :].rearrange("(t p) c -> p t c", p=128)), st_p)
        rmsnorm_to_hT(P_NFW, xt, xfull, hT)
        yield
        ffs = [get() for _ in range(NF // 2)]
        ffT = []
        for f in range(NF):
            sl_ = bf(ffs[f // 2])
            ffT.append(T(sl_.buf, sl_.ap[:, (f % 2) * TB:(f % 2 + 1) * TB]))
        for f in range(NF):
            tg, tu = ring_load(T_GU + 2 * f), ring_load(T_GU + 2 * f + 1)
            gb, ub_ = (banks[0], banks[1]) if f % 2 == 0 else (banks[4], banks[5])
            for (tl, bk) in ((tg, gb), (tu, ub_)):
                for k in range(8):
                    PE.op(lambda h, tl=tl, bk=bk, k=k: h.matmul(bk.ap, lhsT=tl.ap[:, k * 128:(k + 1) * 128], rhs=hT[k].ap, start=(k == 0), stop=(k == 7)),
                          [tl, hT[k]], [bk])
                    if k % 4 == 3:
                        yield
            sg = get()
            ACT.op(lambda h, gb=gb, sg=sg: h.activation(out=sg.ap[:, 0:TB], in_=gb.ap, func=AF.Silu), [gb], [sg])
            DVE.op(lambda h, ub_=ub_, sg=sg, f=f: h.tensor_tensor(out=ffT[f].ap, in0=ub_.ap, in1=sg.ap[:, 0:TB], op=ALU.mult), [ub_, sg], [ffT[f]])
            put(sg)
            yield
        yield from tokproj_g(ffT, NF, T_DOWN, [0, 1, 4, 5], resid_add)
        put(*ffs)
        yield
        rmsnorm_to_hT(P_NPW, xt, xfull, hT)
        ptb = bbf(0)
        for tt in range(4):
            for k in range(2):
                PE.op(lambda h, tt=tt, k=k: h.transpose(out=ptb.ap[:, k * 512 + tt * 128:k * 512 + (tt + 1) * 128], in_=pb.ap[:, tt, k * 128:(k + 1) * 128],
                                                        identity=identb.ap), [pb, identb], [ptb])
        for k in range(2):
            DVE.op(lambda h, k=k: h.tensor_copy(out=pT[k].ap, in_=ptb.ap[:, k * 512:(k + 1) * 512]), [ptb], [pT[k]])
        yield
        sig = {}

        def ep_gate(hf, tt, a):
            s_ = get()
            sig[(hf, tt)] = s_
            ACT.op(lambda h: h.activation(out=s_.ap[:, 0:TB], in_=a.ap, func=AF.Sigmoid), [a], [s_])


        def ep_proj(hf, tt, a):
            s_ = sig[(hf, tt)]
            x = xt[tt][hf]
            DVE.op(lambda h: h.tensor_tensor(out=s_.ap[:, 0:TB], in0=a.ap, in1=s_.ap[:, 0:TB], op=ALU.mult), [a, s_], [s_])
            POOL.op(lambda h: h.tensor_tensor(out=x.ap, in0=x.ap, in1=s_.ap[:, 0:TB], op=ALU.add), [x, s_], [x])
            put(s_)

        for hf_ in range(2):
            yield from tokproj_g(hT, 8, T_PG, [0, 1, 4, 5], ep_gate, bias=True, halves=(hf_,))
            yield from tokproj_g(pT, 2, T_PP, [0, 1, 4, 5], ep_proj, halves=(hf_,))
        for tt in range(4):
            ACT.op(lambda h, tt=tt: h.activation(out=junk.ap, in_=xfull[tt], func=AF.Square, accum_out=ssq.ap[:, tt:tt + 1]), xt[tt], [junk, ssq])
        ACT.op(lambda h: h.activation(out=ssq.ap[:, 4:8], in_=ssq.ap[:, 0:4], func=AF.Sqrt, scale=1.0 / D, bias=EPS), [ssq], [ssq])
        DVE.op(lambda h: h.reciprocal(out=ssq.ap[:, 4:8], in_=ssq.ap[:, 4:8]), [ssq], [ssq])
        for tt in range(4):
            o = ot[tt % 2]
            DVE.op(lambda h, tt=tt, o=o: h.scalar_tensor_tensor(out=o.ap, in0=xfull[tt], scalar=ssq.ap[:, 4 + tt:5 + tt], in1=rowb.ap,
                                                                op0=ALU.mult, op1=ALU.mult), xt[tt] + [ssq, rowb], [o])
            POOL.dma(T(out_d.buf, out_d.ap[r0 + tt * 128:r0 + (tt + 1) * 128, :]), o, st_ot[tt % 2])

    def mk_prefetch(xsrc, blk, par):
        return lambda: load_and_norm(xsrc, blk, xt_sets[par], xfull_sets[par], hT_sets[par], (0, 1, 4, 5))

    for b in range(npre):
        par = (b + npre) % 2
        if b + 1 < npre:
            pf = mk_prefetch(xpre, b + 1, 1 - par)
        else:
            pf = mk_prefetch(xmain, 0, 0)
        run_tasks([mixer(xpre, b, False, "one" if b == 0 else None, b == npre - 1, xt_sets[par], xfull_sets[par], hT_sets[par], [[0, 1, 2, 3]],
                         prenormed=(b > 0), prefetch=pf)])
    if npre > 0:
        for j in range(4):
            DVE.op(lambda h, j=j: h.tensor_scalar(out=hstate[j].ap, in0=hstate[j].ap, scalar1=pcol(P_FL), scalar2=None, op0=ALU.mult),
                   [hstate[j], prm], [hstate[j]])
    cv_issue(len(cv_jobs))
    fk0 = "flag" if npre > 0 else "one"
    run_tasks([mixer(xmain, 0, True, fk0, False, xt_sets[0], xfull_sets[0], hTA, [[0, 1, 2, 3]], prenormed=(npre > 0))])
    for b in range(nmain):
        tasks = [zpart(b, xt_sets[b % 2], xfull_sets[b % 2], hTB)]
        wts = [1]
        if b + 1 < nmain:
            tasks.append(mixer(xmain, b + 1, True, None, False, xt_sets[(b + 1) % 2], xfull_sets[(b + 1) % 2], hTA, [[0, 1], [2, 3]], halfm=True))
            wts = [int(os.environ.get('KWZ', '2')), int(os.environ.get('KWA', '1'))]
        run_tasks(tasks, wts)
    barrier()

    with nc.Block() as blk:
        @blk.sync
        def _(h):
            SP.replay(h)

        @blk.tensor
        def _(h):
            PE.replay(h)

        @blk.scalar
        def _(h):
            ACT.replay(h)

        @blk.vector
        def _(h):
            DVE.replay(h)

        @blk.gpsimd
        def _(h):
            POOL.replay(h)
    es.close()
    return nc


def make_consts():
    c = np.zeros((128, NCST), np.float32)
    c[:, C_ID:C_ID + 128] = np.eye(128, dtype=np.float32)
    c[:, C_ONE:C_ONE + 128] = 1.0
    j = np.arange(64)[:, None]
    i = np.arange(64)[None, :]
    c[0:64, C_CM:C_CM + 64] = (i >= j)
    c[0:64, C_SM:C_SM + 64] = (i > j)
    t = np.arange(512)
    c[0:4, C_RM:C_RM + 512] = (t % 64 != 0)[None, :]
    for h in range(4):
        c[h, C_SEL + h * 128:C_SEL + (h + 1) * 128] = 1.0
    return c


def make_params(inp, fl):
    p = np.zeros((128, NPRM), np.float32)
    col = lambda v: np.asarray(v, np.float32).reshape(-1, 128).T
    p[:, P_NMW:P_NMW + 8] = col(inp["norm_mix_w"][0])
    p[:, P_NFW:P_NFW + 8] = col(inp["norm_ffn_w"][0])
    p[:, P_NPW:P_NPW + 8] = col(inp["norm_ple_w"][0])
    caw = np.asarray(inp["conv_a_w"][0], np.float32)
    p[:, P_CAW:P_CAW + 16] = caw.reshape(4, 4, 128).transpose(2, 1, 0).reshape(128, 16)
    p[:, P_CAB:P_CAB + 4] = col(inp["conv_a_b"][0])
    p[:, P_BX:P_BX + 4] = col(inp["rg_bx"][0])
    p[:, P_BA:P_BA + 4] = col(inp["rg_ba"][0])
    p[:, P_LAM:P_LAM + 4] = col(inp["rg_lambda"][0])
    cqw = np.asarray(inp["conv_qkv_w"][0], np.float32)
    p[:, P_CQW:P_CQW + 48] = cqw.reshape(4, 12, 128).transpose(2, 1, 0).reshape(128, 48)
    p[:, P_GNW] = np.asarray(inp["gdn_norm_w"][0], np.float32)
    p[0:4, P_ALOG] = np.asarray(inp["gdn_a_log"][0], np.float32)
    p[0:4, P_DTB] = np.asarray(inp["gdn_dt_bias"][0], np.float32)
    p[:, P_FL] = fl
    p[:, P_FL + 1] = 1.0 - fl
    return p


def make_rgw(inp):
    r = np.zeros((128, 8, 128), np.float32)
    for gi, nm in enumerate(("rg_wx", "rg_wa")):
        w = np.asarray(inp[nm][0], np.float32)
        for j in range(4):
            for s in range(2):
                r[s * 64:(s + 1) * 64, gi * 4 + j, s * 64:(s + 1) * 64] = w[2 * j + s]
    return r.reshape(128, 8 * 128)


_NC_CACHE = {}


def run(inp, B, S, n_cores):
    half = S // 2
    nb = half // TB
    key = (nb, nb)
    if key not in _NC_CACHE:
        _NC_CACHE[key] = build(nb, nb)
    nc = _NC_CACHE[key]
    x = np.asarray(inp["x"], np.float32)
    p = np.asarray(inp["p"], np.float32)[0]
    cstv = make_consts()
    rgw = make_rgw(inp)
    shared = {
        "w_in": np.ascontiguousarray(np.asarray(inp["w_in"][0], np.float32)),
        "w_out": np.ascontiguousarray(np.asarray(inp["w_out"][0], np.float32)),
        "w_gate": np.ascontiguousarray(np.asarray(inp["w_gate"][0], np.float32)),
        "w_up": np.ascontiguousarray(np.asarray(inp["w_up"][0], np.float32)),
        "w_down": np.ascontiguousarray(np.asarray(inp["w_down"][0], np.float32)),
        "w_pg": np.ascontiguousarray(np.asarray(inp["w_ple_gate"][0], np.float32)),
        "w_pp": np.ascontiguousarray(np.asarray(inp["w_ple_proj"][0], np.float32)),
        "cst": cstv,
        "rowb": np.ascontiguousarray(np.broadcast_to(np.asarray(inp["norm_final_w"], np.float32)[None, :], (128, D))),
        "bpg": np.ascontiguousarray(np.asarray(inp["b_ple_gate"][0], np.float32)[None, :]),
        "rgw": rgw,
    }
    in_maps = []
    for core in range(n_cores):
        b, g = core // 2, core % 2
        m = dict(shared)
        m["xpre"] = np.ascontiguousarray(x[b, 0:half]) if g == 1 else np.zeros((half, D), np.float32)
        m["xmain"] = np.ascontiguousarray(x[b, g * half:(g + 1) * half])
        m["pmain"] = np.ascontiguousarray(p[b, g * half:(g + 1) * half])
        m["prm"] = make_params(inp, float(g))
        in_maps.append(m)
    res = run_bass_kernel_spmd(nc, in_maps, core_ids=list(range(n_cores)))
    out = np.zeros((B, S, D), np.float32)
    for core in range(n_cores):
        b, g = core // 2, core % 2
        out[b, g * half:(g + 1) * half] = res.results[core]["out"]
    return out


def kernel(**inputs):
    return run(inputs, 4, 8192, 8)
```

```python
import os
import numpy as np
from contextlib import ExitStack
from collections import deque
import concourse.bass as bass
import concourse.mybir as mybir
from concourse.bass_utils import run_bass_kernel_spmd

F32 = mybir.dt.float32
BF16 = mybir.dt.bfloat16
AF = mybir.ActivationFunctionType
ALU = mybir.AluOpType

D = 1024
INC = 3080
DFF = 2816
NF = 22
PLE = 256
TB = 512
EPS = 1e-6
SW = 520
NSLOT = 48
RD = int(os.environ.get('KRD', '10'))
T_WOUT = 0
T_GU = 8
T_DOWN = 8 + 44
T_PG = T_DOWN + 22
T_PP = T_PG + 8
T_IN = T_PP + 2
NTILE = T_IN + 25
C_ID, C_ONE, C_CM, C_SM, C_RM, C_SEL, NCST = 0, 128, 256, 320, 384, 896, 1408
P_NMW, P_NFW, P_NPW, P_CAW, P_CAB, P_BX, P_BA, P_LAM, P_CQW, P_GNW, P_ALOG, P_DTB, P_FL, NPRM = \
    0, 8, 16, 24, 40, 44, 48, 52, 56, 104, 105, 106, 107, 112


class Buf:
    __slots__ = ("w", "r", "excl", "owner")

    def __init__(self, excl=False):
        self.w = None
        self.r = {}
        self.excl = excl
        self.owner = None


CUR_TASK = [None]


class T:
    __slots__ = ("buf", "ap")

    def __init__(self, buf, ap):
        self.buf = buf
        self.ap = ap

    def __getitem__(self, k):
        return T(self.buf, self.ap[k])


class Stream:
    def __init__(self, sem):
        self.sem = sem
        self.count = 0


class Eng:
    def __init__(self, name, skip_self=False):
        self.name = name
        self.stream = None
        self.seen = {}
        self.prog = []
        self.skip_self = skip_self

    def _deps(self, reads, writes):
        need = {}
        for t in reads:
            w = t.buf.w
            if w is not None and need.get(w[0], 0) < w[1]:
                need[w[0]] = w[1]
            if t.buf.excl:
                for s, v in t.buf.r.items():
                    if s is not self.stream and need.get(s, 0) < v:
                        need[s] = v
        for t in writes:
            b = t.buf
            if b.w is not None and need.get(b.w[0], 0) < b.w[1]:
                need[b.w[0]] = b.w[1]
            for s, v in b.r.items():
                if need.get(s, 0) < v:
                    need[s] = v
        for s, v in need.items():
            if s is self.stream and self.skip_self:
                continue
            if self.seen.get(s, 0) < v:
                self.prog.append(("w", s.sem, v))
                self.seen[s] = v

    def op(self, fn, reads, writes):
        self._deps(reads, writes)
        st = self.stream
        st.count += 1
        c = st.count
        self.prog.append(("o", fn, st.sem, 1))
        for t in reads:
            t.buf.r[st] = c
            if t.buf.excl and t.buf.owner is not CUR_TASK[0]:
                raise RuntimeError("PSUM bank read by a task that did not write it last (interleaving bug)")
        for t in writes:
            t.buf.w = (st, c)
            t.buf.r = {}
            t.buf.owner = CUR_TASK[0]

    def dma(self, out, in_, dstream, xr=(), xw=()):
        rs, ws = [in_] + list(xr), [out] + list(xw)
        self._deps(rs, ws)
        dstream.count += 16
        c = dstream.count
        oa, ia = out.ap, in_.ap
        self.prog.append(("o", lambda h: h.dma_start(out=oa, in_=ia), dstream.sem, 16))
        for t in rs:
            t.buf.r[dstream] = c
        for t in ws:
            t.buf.w = (dstream, c)
            t.buf.r = {}

    def wait_stream(self, s):
        if s.count > 0 and self.seen.get(s, 0) < s.count:
            self.prog.append(("w", s.sem, s.count))
            self.seen[s] = s.count

    def replay(self, h):
        for it in self.prog:
            if it[0] == "w":
                h.wait_ge(it[1], it[2])
            else:
                it[1](h).then_inc(it[2], it[3])


def build(npre, nmain, debug=False):
    KSTOP = os.environ.get('KSTOP', '')
    nc = bass.Bass("TRN2", target_bir_lowering=False)
    es = ExitStack()

    def dram(name, shape, dt, kind="ExternalInput"):
        return T(Buf(), nc.dram_tensor(name, shape, dt, kind=kind).ap())

    def sb(name, shape, dt):
        return T(Buf(), es.enter_context(nc.sbuf_tensor(name, shape, dt))[:])

    TPRE, TMAIN = max(npre, 1) * TB, nmain * TB
    xpre = dram("xpre", [TPRE, D], F32)
    xmain = dram("xmain", [TMAIN, D], F32)
    pmain = dram("pmain", [TMAIN, PLE], F32)
    w_in = dram("w_in", [D, INC], F32)
    w_out = dram("w_out", [D, D], F32)
    w_gate = dram("w_gate", [D, DFF], F32)
    w_up = dram("w_up", [D, DFF], F32)
    w_down = dram("w_down", [DFF, D], F32)
    w_pg = dram("w_pg", [D, D], F32)
    w_pp = dram("w_pp", [PLE, D], F32)
    cst_d = dram("cst", [128, NCST], F32)
    prm_d = dram("prm", [128, NPRM], F32)
    rowb_d = dram("rowb", [128, D], F32)
    bpg_d = dram("bpg", [1, D], F32)
    rgw_d = dram("rgw", [128, 8 * 128], F32)
    out_d = dram("out", [TMAIN, D], F32, kind="ExternalOutput")
    scr = dram("scr", [NTILE, 128, 1024], BF16, kind="Internal")

    PE, ACT, DVE, POOL, SP = Eng("pe", True), Eng("act"), Eng("dve"), Eng("pool"), Eng("sp")
    engs = [PE, ACT, DVE, POOL, SP]
    for e in engs:
        e.stream = Stream(es.enter_context(nc.semaphore("s_" + e.name)))
    streams = [e.stream for e in engs[:4]]

    def mkstream(name):
        st = Stream(es.enter_context(nc.semaphore(name)))
        streams.append(st)
        return st

    st_ring = [mkstream("s_rg%d" % i) for i in range(RD)]
    st_x = [mkstream("s_x%d" % i) for i in range(4)]
    st_p = mkstream("s_p")
    st_ot = [mkstream("s_ot%d" % i) for i in range(2)]
    st_c = [mkstream("s_c%d" % i) for i in range(3)]

    def barrier():
        for e in engs:
            for s in streams:
                e.wait_stream(s)

    ring_t = sb("ring", [128, RD, 1024], BF16)
    ring = [T(Buf(), ring_t.ap[:, i, :]) for i in range(RD)]
    xt_sets, xfull_sets = [], []
    for par in range(2):
        xt_t = sb("xt%d" % par, [128, 4, D], F32)
        xt_sets.append([[T(Buf(), xt_t.ap[:, tt, hf * 512:(hf + 1) * 512]) for hf in range(2)] for tt in range(4)])
        xfull_sets.append([xt_t.ap[:, tt, :] for tt in range(4)])
    hT_sets = []
    for par in range(2):
        hT_t = sb("hT%d" % par, [128, 8, TB], BF16)
        hT_sets.append([T(Buf(), hT_t.ap[:, k, :]) for k in range(8)])
    hTA, hTB = hT_sets
    xnb_t = sb("xnb", [128, 2, D], BF16)
    xnb = [T(Buf(), xnb_t.ap[:, i, :]) for i in range(2)]
    junk = sb("junk", [128, D], BF16)
    yT_t = sb("yT", [128, 8, TB], BF16)
    yT = [T(Buf(), yT_t.ap[:, k, :]) for k in range(8)]
    cst = sb("cstsb", [128, NCST], F32)
    prm = sb("prmsb", [128, NPRM], F32)
    der = sb("der", [128, 16], F32)
    rowb = sb("rowbsb", [128, D], F32)
    bpgb = sb("bpgb", [1, D], BF16)
    rgwb = sb("rgwb", [128, 8 * 128], BF16)
    identb = sb("identb", [128, 128], BF16)
    onesb = sb("onesb", [128, 128], BF16)
    S_t = sb("S", [128, 4, 128], F32)
    Sb_t = sb("Sb", [128, 4, 128], BF16)
    Sf = [T(Buf(), S_t.ap[:, h, :]) for h in range(4)]
    Sb = [T(Buf(), Sb_t.ap[:, h, :]) for h in range(4)]
    hist = sb("hist", [128, 16, 3], F32)
    hists = [T(Buf(), hist.ap[:, i, :]) for i in range(16)]
    hst_t = sb("hstate", [128, 4], F32)
    hstate = [T(Buf(), hst_t.ap[:, j:j + 1]) for j in range(4)]
    ssq = sb("ssq", [128, 8], F32)
    cols = sb("cols", [64, 5, 8, 4], F32)
    gtot = sb("gtot", [128, 4, 8], F32)
    small = sb("smallsb", [64, 4, 128], F32)
    smallb = sb("smallb", [64, 8, 128], BF16)
    Rb = [T(Buf(), smallb.ap[:, i, :]) for i in range(4)]
    vnb = [T(Buf(), smallb.ap[:, 4 + i, :]) for i in range(4)]
    tmpf = [T(Buf(), small.ap[:, i, :]) for i in range(4)]
    pb = sb("pbb", [128, 4, PLE], BF16)
    pT_t = sb("pT", [128, 2, TB], BF16)
    pT = [T(Buf(), pT_t.ap[:, k, :]) for k in range(2)]
    ot_t = sb("ot", [128, 2, D], F32)
    ot = [T(Buf(), ot_t.ap[:, i, :]) for i in range(2)]
    rem = nc.sbuf_bytes_remaining
    rem = rem() if callable(rem) else rem
    NSLOT = (rem - 512) // (SW * 4)
    arena = sb("arena", [128, NSLOT * SW], F32)
    print('NSLOT', NSLOT)
    slots = [T(Buf(), arena.ap[:, i * SW:(i + 1) * SW]) for i in range(NSLOT)]
    free = deque(slots)

    def get():
        return free.popleft()

    def put(*ts):
        for t in ts:
            free.append(T(t.buf, arena.ap[:, 0:SW]) if False else t)

    def bf(t, n=2 * SW):
        return T(t.buf, t.ap.bitcast(BF16))

    banks = []
    for i in range(8):
        pt_ = es.enter_context(nc.psum_tensor("pb%d" % i, [128, 512], F32))
        banks.append(T(Buf(excl=True), pt_[:]))

    def bbf(i):
        return T(banks[i].buf, banks[i].ap.bitcast(BF16))

    def cc(col, n=1):
        return cst.ap[:, col:col + n]

    def pcol(col):
        return prm.ap[:, col:col + 1]

    ident = cst.ap[:, C_ID:C_ID + 128]
    onesf = cst.ap[:, C_ONE:C_ONE + 128]

    rr = [0]

    def evac_eng():
        rr[0] ^= 1
        return ACT if rr[0] else DVE

    SP.dma(cst, cst_d, st_c[0])
    SP.dma(prm, prm_d, st_c[1])
    SP.dma(rowb, rowb_d, st_c[2])
    DVE.op(lambda h: h.tensor_copy(out=identb.ap, in_=ident), [cst], [identb])
    DVE.op(lambda h: h.tensor_copy(out=onesb.ap, in_=onesf), [cst], [onesb])
    DVE.op(lambda h: h.memset(S_t.ap, 0.0), [], Sf)
    DVE.op(lambda h: h.memset(Sb_t.ap, 0.0), [], Sb)
    DVE.op(lambda h: h.memset(hist.ap, 0.0), [], hists)
    DVE.op(lambda h: h.memset(hst_t.ap, 0.0), [], hstate)
    ACT.op(lambda h: h.activation(out=der.ap[:, 0:4], in_=prm.ap[:, P_LAM:P_LAM + 4], func=AF.Exp, scale=-1.0), [prm], [der])
    ACT.op(lambda h: h.activation(out=der.ap[:, 0:4], in_=der.ap[:, 0:4], func=AF.Ln, bias=1.0), [der], [der])
    DVE.op(lambda h: h.tensor_scalar(out=der.ap[:, 4:8], in0=der.ap[:, 0:4], scalar1=-16.0, scalar2=None, op0=ALU.mult), [der], [der])
    DVE.op(lambda h: h.tensor_scalar(out=der.ap[:, 0:4], in0=der.ap[:, 0:4], scalar1=-8.0, scalar2=None, op0=ALU.mult), [der], [der])
    ACT.op(lambda h: h.activation(out=der.ap[0:4, 8:9], in_=prm.ap[0:4, P_ALOG:P_ALOG + 1], func=AF.Exp), [prm], [der])
    DVE.op(lambda h: h.tensor_scalar(out=der.ap[0:4, 8:9], in0=der.ap[0:4, 8:9], scalar1=-1.0, scalar2=None, op0=ALU.mult), [der], [der])

    st_cva, st_cvb, st_cvc, st_cvd = mkstream("s_cva"), mkstream("s_cvb"), mkstream("s_cvc"), mkstream("s_cvd")
    scr_in = T(Buf(), scr.ap)
    POOL.dma(rgwb, rgw_d, st_cvc)
    POOL.dma(bpgb, bpg_d, st_cvd)
    for k in range(8):
        view = scr.ap[T_IN:T_IN + 24].rearrange("f p (k c) -> p f k c", k=8)[:, :, k, :]
        POOL.dma(T(scr_in.buf, view), T(w_in.buf, w_in.ap[k * 128:(k + 1) * 128, 0:3072].rearrange("p (f c) -> p f c", c=128)), st_cva)
        POOL.dma(T(scr_in.buf, scr.ap[T_IN + 24].rearrange("p (k c) -> p k c", k=8)[:, k, 0:8]), T(w_in.buf, w_in.ap[k * 128:(k + 1) * 128, 3072:3080]), st_cva)

    cv_jobs = []

    def pair_tiles(wd, K, base):
        for k in range(K):
            for hf in range(2):
                tix = base + hf * (K // 2) + k // 2
                dst = scr.ap[tix].rearrange("p (i d) -> p i d", i=2)[:, k % 2, :]
                cv_jobs.append((T(scr.buf, dst), T(wd.buf, wd.ap[k * 128:(k + 1) * 128, hf * 512:(hf + 1) * 512])))

    pair_tiles(w_out, 8, T_WOUT)
    for k in range(8):
        for gi, wd in enumerate((w_gate, w_up)):
            view = scr.ap[T_GU:T_GU + 44].rearrange("(f two) p (k c) -> two p f k c", two=2, k=8)[gi][:, :, k, :]
            cv_jobs.append((T(scr.buf, view), T(wd.buf, wd.ap[k * 128:(k + 1) * 128, :].rearrange("p (f c) -> p f c", c=128))))
    pair_tiles(w_down, NF, T_DOWN)
    pair_tiles(w_pg, 8, T_PG)
    pair_tiles(w_pp, 2, T_PP)
    cv_total = len(cv_jobs)

    def cv_issue(n):
        for _ in range(min(n, len(cv_jobs))):
            o_, i_ = cv_jobs.pop(0)
            POOL.dma(o_, i_, st_cvb)

    ringn = [0]

    def ring_load(tix):
        s = ring[ringn[0] % RD]
        SP.dma(s, T((scr_in if tix >= T_IN else scr).buf, scr.ap[tix]), st_ring[ringn[0] % RD])
        ringn[0] += 1
        return s

    def rmsnorm_to_hT(wcol0, xt, xfull, hT, nb=(0, 1, 4, 5)):
        for tt in range(4):
            ACT.op(lambda h, tt=tt: h.activation(out=junk.ap, in_=xfull[tt], func=AF.Square, accum_out=ssq.ap[:, tt:tt + 1]),
                   xt[tt], [junk, ssq])
        ACT.op(lambda h: h.activation(out=ssq.ap[:, 4:8], in_=ssq.ap[:, 0:4], func=AF.Sqrt, scale=1.0 / D, bias=EPS), [ssq], [ssq])
        DVE.op(lambda h: h.reciprocal(out=ssq.ap[:, 4:8], in_=ssq.ap[:, 4:8]), [ssq], [ssq])
        if KSTOP == 'nrm1':
            return
        for tt in range(4):
            xb = xnb[tt % 2]
            if tt % 2 == 0:
                ACT.op(lambda h, tt=tt, xb=xb: h.activation(out=xb.ap, in_=xfull[tt], func=AF.Copy, scale=ssq.ap[:, 4 + tt:5 + tt]),
                       xt[tt] + [ssq], [xb])
            else:
                DVE.op(lambda h, tt=tt, xb=xb: h.tensor_scalar(out=xb.ap, in0=xfull[tt], scalar1=ssq.ap[:, 4 + tt:5 + tt], scalar2=None, op0=ALU.mult),
                       xt[tt] + [ssq], [xb])
            for k in range(8):
                pk = bbf(nb[k // 2])
                PE.op(lambda h, k=k, tt=tt, xb=xb, pk=pk: h.transpose(
                    out=pk.ap[:, (k % 2) * 512 + tt * 128:(k % 2) * 512 + (tt + 1) * 128],
                    in_=xb.ap[:, k * 128:(k + 1) * 128], identity=identb.ap), [xb, identb], [pk])
        if KSTOP == 'nrm2':
            return
        for k in range(8):
            pk = bbf(nb[k // 2])
            src = pk.ap[:, (k % 2) * 512:(k % 2 + 1) * 512]
            e = evac_eng() if os.environ.get('KEV', '') == '' else (DVE if os.environ['KEV'] == 'dve' else ACT)
            if e is ACT:
                e.op(lambda h, k=k, src=src: h.activation(out=hT[k].ap, in_=src, func=AF.Identity, scale=pcol(wcol0 + k)), [pk, prm], [hT[k]])
            else:
                e.op(lambda h, k=k, src=src: h.tensor_scalar(out=hT[k].ap, in0=src, scalar1=pcol(wcol0 + k), scalar2=None, op0=ALU.mult),
                     [pk, prm], [hT[k]])

    def run_tasks(tasks, weights=None):
        tasks = list(tasks)
        weights = dict(zip(tasks, weights)) if weights else {}
        while tasks:
            for t_ in list(tasks):
                for _ in range(weights.get(t_, 1)):
                    prev = CUR_TASK[0]
                    CUR_TASK[0] = t_
                    try:
                        next(t_)
                    except StopIteration:
                        tasks.remove(t_)
                        CUR_TASK[0] = prev
                        break
                    CUR_TASK[0] = prev

    def rrobin_sync(groups):
        parked = []
        for grp in groups:
            active = list(grp)
            while active:
                for t_ in list(active):
                    prev = CUR_TASK[0]
                    CUR_TASK[0] = t_
                    try:
                        v = next(t_)
                    except StopIteration:
                        v = None
                        active.remove(t_)
                    CUR_TASK[0] = prev
                    if v == "SYNC":
                        active.remove(t_)
                        parked.append(t_)
                    yield
        yield from rrobin(parked)

    def rrobin(tasks):
        tasks = list(tasks)
        while tasks:
            for t_ in list(tasks):
                prev = CUR_TASK[0]
                CUR_TASK[0] = t_
                try:
                    next(t_)
                except StopIteration:
                    tasks.remove(t_)
                CUR_TASK[0] = prev
                yield

    ipb = [0]

    def inproj_g(c0, m, bank, hT, rot=(0, 1, 4, 5)):
        if bank is None:
            bk = banks[rot[ipb[0] % 4]]
            ipb[0] += 1
        else:
            bk = banks[bank]
        ch = min(c0 // 128, 24)
        off = c0 - ch * 128
        tl = ring_load(T_IN + ch)
        for k in range(8):
            PE.op(lambda h, k=k, bk=bk, tl=tl: h.matmul(bk.ap[0:m, :], lhsT=tl.ap[:, k * 128 + off:k * 128 + off + m], rhs=hT[k].ap,
                                                        start=(k == 0), stop=(k == 7)), [tl, hT[k]], [bk])
        return T(bk.buf, bk.ap[0:m, :])

    def conv(ps, hidx, wbase, bias_ap):
        ext, c = get(), get()
        hs = hists[hidx]
        DVE.op(lambda h: h.tensor_copy(out=ext.ap[:, 0:3], in_=hs.ap), [hs], [ext])
        ACT.op(lambda h: h.activation(out=ext.ap[:, 3:3 + TB], in_=ps.ap, func=AF.Copy), [ps], [ext])
        if bias_ap is None:
            ACT.op(lambda h: h.activation(out=c.ap[:, 0:TB], in_=ps.ap, func=AF.Copy, scale=pcol(wbase + 3)), [ps, prm], [c])
        else:
            ACT.op(lambda h: h.activation(out=c.ap[:, 0:TB], in_=ps.ap, func=AF.Identity, scale=pcol(wbase + 3), bias=bias_ap), [ps, prm], [c])
        DVE.op(lambda h: h.tensor_copy(out=hs.ap, in_=ext.ap[:, TB:TB + 3]), [ext], [hs])
        for k in (2, 1, 0):
            DVE.op(lambda h, k=k: h.scalar_tensor_tensor(out=c.ap[:, 0:TB], in0=ext.ap[:, k:k + TB], scalar=pcol(wbase + k), in1=c.ap[:, 0:TB],
                                                         op0=ALU.mult, op1=ALU.add), [ext, c, prm], [c])
        put(ext)
        return c

    def tokproj_g(src, K, base, accs, epilogue, bias=False, halves=(0, 1)):
        for hf in halves:
            for kp in range(K // 2):
                tl = ring_load(base + hf * (K // 2) + kp)
                for i in range(2):
                    k = 2 * kp + i
                    for tt in range(4):
                        a = banks[accs[tt]]
                        first = (k == 0) and not bias
                        if k == 0 and bias:
                            PE.op(lambda h, a=a, hf=hf: h.matmul(a.ap, lhsT=onesb.ap[0:1, :], rhs=bpgb.ap[0:1, hf * 512:(hf + 1) * 512],
                                                                 start=True, stop=False), [onesb, bpgb], [a])
                        PE.op(lambda h, a=a, k=k, i=i, tt=tt, tl=tl, first=first: h.matmul(
                            a.ap, lhsT=src[k].ap[:, tt * 128:(tt + 1) * 128], rhs=tl.ap[:, i * 512:(i + 1) * 512],
                            start=first, stop=(k == K - 1)), [src[k], tl], [a])
                yield
            for tt in range(4):
                epilogue(hf, tt, banks[accs[tt]])
            yield

    def tokproj(*a_, **k_):
        for _ in tokproj_g(*a_, **k_):
            pass

    def mk_resid(xt):
        def resid_add(hf, tt, a):
            x = xt[tt][hf]
            DVE.op(lambda h: h.tensor_tensor(out=x.ap, in0=a.ap, in1=x.ap, op=ALU.add), [a, x], [x])
        return resid_add

    def load_and_norm(xsrc, blk, xt, xfull, hT, nb):
        r0 = blk * TB
        for tt in range(4):
            SP.dma(T(xt[tt][0].buf, xfull[tt]), T(xsrc.buf, xsrc.ap[r0 + tt * 128:r0 + (tt + 1) * 128, :]), st_x[tt], xw=[xt[tt][1]])
        rmsnorm_to_hT(P_NMW, xt, xfull, hT, nb)

    def mixer(xsrc, blk, main, first_kind, last_pre, xt, xfull, hT, hgroups, halfm=False, prenormed=False, prefetch=None):
        r0 = blk * TB
        resid_add = mk_resid(xt)

        abanks = (2, 3, 6, 7) if halfm else (0, 1, 4, 5)
        hbank = [None]

        def inproj(c0, m=128):
            return inproj_g(c0, m, hbank[0], hT, abanks)

        if not main:
            cv_issue(-(-cv_total // max(npre, 1)))
        if not prenormed:
            load_and_norm(xsrc, blk, xt, xfull, hT, abanks)
        yield

        def rg_task():
            us, gxs, gas, as_ = [], [], [], []
            for j in range(4):
                ps = inproj(j * 128)
                us.append(conv(ps, j, P_CAW + 4 * j, pcol(P_CAB + j)))
            yield
            for j in range(4):
                u = us[j]
                ubs = get()
                ub = bf(ubs)
                POOL.op(lambda h, u=u, ub=ub: h.tensor_copy(out=ub.ap[:, 0:TB], in_=u.ap[:, 0:TB]), [u], [ub])
                gx, ga = get(), get()
                for (dstt, wofs, bcol_) in ((gx, j, P_BX + j), (ga, 4 + j, P_BA + j)):
                    bk = banks[2 + (wofs // 4)]
                    PE.op(lambda h, bk=bk, wofs=wofs, ub=ub: h.matmul(bk.ap, lhsT=rgwb.ap[:, wofs * 128:(wofs + 1) * 128], rhs=ub.ap[:, 0:TB],
                                                                      start=True, stop=True), [rgwb, ub], [bk])
                    ACT.op(lambda h, bk=bk, dstt=dstt, bcol_=bcol_: h.activation(out=dstt.ap[:, 0:TB], in_=bk.ap, func=AF.Sigmoid, bias=pcol(bcol_)),
                           [bk, prm], [dstt])
                put(ubs)
                gxs.append(gx)
                gas.append(ga)
            yield
            for j in range(4):
                a, ga = get(), gas[j]
                ACT.op(lambda h, a=a, ga=ga, j=j: h.activation(out=a.ap[:, 0:TB], in_=ga.ap[:, 0:TB], func=AF.Exp, scale=der.ap[:, j:j + 1]), [ga, der], [a])
                ACT.op(lambda h, ga=ga, j=j: h.activation(out=ga.ap[:, 0:TB], in_=ga.ap[:, 0:TB], func=AF.Exp, scale=der.ap[:, 4 + j:5 + j]), [ga, der], [ga])
                as_.append(a)
            yield
            for j in range(4):
                ga = gas[j]
                ACT.op(lambda h, ga=ga: h.activation(out=ga.ap[:, 0:TB], in_=ga.ap[:, 0:TB], func=AF.Sqrt, scale=-1.0, bias=1.0), [ga], [ga])
                if first_kind == "one":
                    DVE.op(lambda h, ga=ga: h.memset(ga.ap[:, 0:1], 1.0), [], [ga])
                elif first_kind == "flag":
                    DVE.op(lambda h, ga=ga: h.tensor_scalar(out=ga.ap[:, 0:1], in0=ga.ap[:, 0:1], scalar1=pcol(P_FL), scalar2=pcol(P_FL + 1),
                                                            op0=ALU.mult, op1=ALU.add), [ga, prm], [ga])
            yield
            hhs = []
            yield
            for j in range(4):
                u, gx, mu, a = us[j], gxs[j], gas[j], as_[j]
                POOL.op(lambda h, u=u, gx=gx: h.tensor_tensor(out=gx.ap[:, 0:TB], in0=u.ap[:, 0:TB], in1=gx.ap[:, 0:TB], op=ALU.mult), [u, gx], [gx])
                POOL.op(lambda h, mu=mu, gx=gx: h.tensor_tensor(out=gx.ap[:, 0:TB], in0=gx.ap[:, 0:TB], in1=mu.ap[:, 0:TB], op=ALU.mult), [mu, gx], [gx])
                hh = u
                DVE.op(lambda h, hh=hh, a=a, gx=gx, j=j: h.tensor_tensor_scan(out=hh.ap[:, 0:TB], data0=a.ap[:, 0:TB], data1=gx.ap[:, 0:TB],
                                                                              initial=hstate[j].ap, op0=ALU.mult, op1=ALU.add),
                       [a, gx, hstate[j]], [hh])
                DVE.op(lambda h, hh=hh, j=j: h.tensor_copy(out=hstate[j].ap, in_=hh.ap[:, TB - 1:TB]), [hh], [hstate[j]])
                put(a, mu)
                hhs.append(hh)
            yield
            for j in range(4):
                hh, gx = hhs[j], gxs[j]
                if main:
                    ps = inproj(512 + j * 128)
                    ACT.op(lambda h, ps=ps, gx=gx: h.activation(out=gx.ap[:, 0:TB], in_=ps.ap, func=AF.Gelu_apprx_tanh), [ps], [gx])
                    POOL.op(lambda h, hh=hh, gx=gx, j=j: h.tensor_tensor(out=yT[j].ap, in0=hh.ap[:, 0:TB], in1=gx.ap[:, 0:TB], op=ALU.mult), [hh, gx], [yT[j]])
                put(hh, gx)


        G = {}

        def sc_task():
            bp = inproj(3072, 4)
            brow = get()
            ACT.op(lambda h: h.activation(out=brow.ap[0:4, 0:TB], in_=bp.ap, func=AF.Sigmoid), [bp], [brow])
            yield
            apz = inproj(3076, 4)
            grow, gcrow, kdrow = get(), get(), get()
            ACT.op(lambda h: h.activation(out=grow.ap[0:4, 0:TB], in_=apz.ap, func=AF.Exp, bias=prm.ap[0:4, P_DTB:P_DTB + 1]), [apz, prm], [grow])
            ACT.op(lambda h: h.activation(out=grow.ap[0:4, 0:TB], in_=grow.ap[0:4, 0:TB], func=AF.Ln, bias=1.0), [grow], [grow])
            DVE.op(lambda h: h.tensor_scalar(out=grow.ap[0:4, 0:TB], in0=grow.ap[0:4, 0:TB], scalar1=der.ap[0:4, 8:9], scalar2=None, op0=ALU.mult),
                   [grow, der], [grow])
            DVE.op(lambda h: h.tensor_tensor_scan(out=gcrow.ap[0:4, 0:TB], data0=cst.ap[0:4, C_RM:C_RM + TB], data1=grow.ap[0:4, 0:TB],
                                                  initial=0.0, op0=ALU.mult, op1=ALU.add), [cst, grow], [gcrow])
            yield
            g3 = gcrow.ap[0:4, 0:TB].rearrange("p (c j) -> p c j", j=64)
            DVE.op(lambda h: h.tensor_tensor(out=kdrow.ap[0:4, 0:TB].rearrange("p (c j) -> p c j", j=64), in0=g3[:, :, 63:64].broadcast_to([4, 8, 64]),
                                             in1=g3, op=ALU.subtract), [gcrow], [kdrow])
            colp = banks[3]
            cp4 = colp.ap[0:64, 0:96].rearrange("p (a c h) -> p a c h", a=3, c=8)
            for ai, rowt in enumerate((gcrow, brow, kdrow)):
                for c in range(8):
                    PE.op(lambda h, ai=ai, rowt=rowt, c=c: h.transpose(out=cp4[:, ai, c, :], in_=rowt.ap[0:4, c * 64:(c + 1) * 64], identity=ident[0:4, 0:4]),
                          [rowt, cst], [colp])
            ACT.op(lambda h: h.activation(out=cols.ap[:, 0], in_=cp4[:, 0], func=AF.Exp), [colp], [cols])
            ACT.op(lambda h: h.activation(out=cols.ap[:, 1], in_=cp4[:, 2], func=AF.Exp), [colp], [cols])
            DVE.op(lambda h: h.tensor_copy(out=cols.ap[:, 2], in_=cp4[:, 1]), [colp], [cols])
            DVE.op(lambda h: h.tensor_copy(out=cols.ap[:, 4], in_=cp4[:, 0]), [colp], [cols])
            DVE.op(lambda h: h.tensor_tensor(out=cols.ap[:, 3], in0=cols.ap[:, 2], in1=cols.ap[:, 0], op=ALU.mult), [cols], [cols])
            put(grow, kdrow)
            G['brow'], G['gcrow'] = brow, gcrow


        def head(hd):
            brow, gcrow = G['brow'], G['gcrow']

            def hinproj(c0_):
                hbank[0] = myip
                r_ = inproj(c0_)
                hbank[0] = None
                return r_

            odd = hd % 2
            if halfm:
                ba, bb_ = (6, 7) if odd else (2, 3)
                sml = banks[bb_]
                vset = (ba, bb_, ba)
                ssbank = ba
                gcb, bbk = banks[ba], sml
                myip = bb_
            else:
                sml = banks[7] if odd else banks[3]
                vset = (0, 1, 2) if odd else (4, 5, 6)
                ssbank = (2, 6, 3, 7)[hd]
                gcb, bbk = banks[2], sml
                myip = None
            PE.op(lambda h, hd=hd: h.matmul(gcb.ap, lhsT=cst.ap[0:4, C_SEL + hd * 128:C_SEL + (hd + 1) * 128], rhs=gcrow.ap[0:4, 0:TB],
                                            start=True, stop=True), [cst, gcrow], [gcb])
            PE.op(lambda h, hd=hd: h.matmul(bbk.ap[0:64, :], lhsT=cst.ap[0:4, C_SEL + hd * 128:C_SEL + hd * 128 + 64], rhs=brow.ap[0:4, 0:TB],
                                            start=True, stop=True), [cst, brow], [bbk])
            gambc, E, bm = get(), get(), get()
            gcb3 = gcb.ap.rearrange("p (c j) -> p c j", j=64)
            if main:
                ACT.op(lambda h: h.activation(out=gambc.ap[:, 0:TB], in_=gcb.ap, func=AF.Exp), [gcb], [gambc])
            ACT.op(lambda h, hd=hd: h.activation(out=gtot.ap[:, hd, :], in_=gcb3[:, :, 63], func=AF.Exp), [gcb], [gtot])
            E3 = E.ap[0:64, 0:TB].rearrange("p (c j) -> p c j", j=64)
            DVE.op(lambda h, hd=hd: h.scalar_tensor_tensor(out=E3, in0=gcb3[0:64], scalar=0.0,
                                                           in1=cols.ap[:, 4, :, hd].unsqueeze(2).broadcast_to([64, 8, 64]),
                                                           op0=ALU.add, op1=ALU.subtract), [gcb, cols], [E])
            DVE.op(lambda h: h.tensor_scalar(out=E.ap[0:64, 0:TB], in0=E.ap[0:64, 0:TB], scalar1=0.0, scalar2=None, op0=ALU.min), [E], [E])
            ACT.op(lambda h: h.activation(out=E.ap[0:64, 0:TB], in_=E.ap[0:64, 0:TB], func=AF.Exp), [E], [E])
            sm3 = cst.ap[0:64, C_SM:C_SM + 64].unsqueeze(1).broadcast_to([64, 8, 64])
            cm3 = cst.ap[0:64, C_CM:C_CM + 64].unsqueeze(1).broadcast_to([64, 8, 64])
            bm3 = bm.ap[0:64, 0:TB].rearrange("p (c j) -> p c j", j=64)
            DVE.op(lambda h: h.tensor_tensor(out=bm3, in0=bbk.ap[0:64, :].rearrange("p (c j) -> p c j", j=64), in1=sm3, op=ALU.mult), [bbk, cst], [bm])
            POOL.op(lambda h: h.tensor_tensor(out=bm.ap[0:64, 0:TB], in0=bm.ap[0:64, 0:TB], in1=E.ap[0:64, 0:TB], op=ALU.mult), [bm, E], [bm])
            if main:
                POOL.op(lambda h: h.tensor_tensor(out=E3, in0=E3, in1=cm3, op=ALU.mult), [E, cst], [E])
            GT, DTc = bm, E
            yield

            qkn = get()
            qknb = bf(qkn)
            qd = get()
            qdb = bf(qd)
            which = (("q", 8 + hd, 0), ("k", 12 + hd, 512)) if main else (("k", 12 + hd, 512),)
            for (nm, ch, off) in which:
                ps = hinproj(ch * 128)
                c = conv(ps, 4 + (ch - 8), P_CQW + 4 * (ch - 8), None)
                ACT.op(lambda h, c=c: h.activation(out=c.ap[:, 0:TB], in_=c.ap[:, 0:TB], func=AF.Silu), [c], [c])
                sq = get()
                ACT.op(lambda h, c=c, sq=sq: h.activation(out=sq.ap[:, 0:TB], in_=c.ap[:, 0:TB], func=AF.Square), [c], [sq])
                yield
                ssb = banks[ssbank]
                PE.op(lambda h, sq=sq, ssb=ssb: h.matmul(ssb.ap, lhsT=onesf, rhs=sq.ap[:, 0:TB], start=True, stop=True), [cst, sq], [ssb])
                yield
                ACT.op(lambda h, sq=sq, ssb=ssb: h.activation(out=sq.ap[:, 0:TB], in_=ssb.ap, func=AF.Ln, bias=EPS), [ssb], [sq])
                ACT.op(lambda h, sq=sq: h.activation(out=sq.ap[:, 0:TB], in_=sq.ap[:, 0:TB], func=AF.Exp, scale=-0.5), [sq], [sq])
                scl = (128.0 ** -0.5) if nm == "q" else 1.0
                DVE.op(lambda h, c=c, sq=sq, off=off, scl=scl: h.scalar_tensor_tensor(out=qknb.ap[:, off:off + TB], in0=c.ap[:, 0:TB], scalar=scl,
                                                                                   in1=sq.ap[:, 0:TB], op0=ALU.mult, op1=ALU.mult), [c, sq], [qknb])
                put(c, sq)
                yield
            if main:
                POOL.op(lambda h: h.tensor_tensor(out=qdb.ap[:, 0:TB], in0=qknb.ap[:, 0:TB], in1=gambc.ap[:, 0:TB], op=ALU.mult), [qknb, gambc], [qdb])
            elif last_pre:
                ps = hinproj((8 + hd) * 128)
                ACT.op(lambda h, ps=ps, hd=hd: h.activation(out=hists[4 + hd].ap, in_=ps.ap[:, TB - 3:TB], func=AF.Copy), [ps], [hists[4 + hd]])
            put(gambc)
            ps = hinproj((16 + hd) * 128)
            vc = conv(ps, 4 + 8 + hd, P_CQW + 4 * (8 + hd), None)
            ACT.op(lambda h: h.activation(out=vc.ap[:, 0:TB], in_=vc.ap[:, 0:TB], func=AF.Silu), [vc], [vc])
            yield
            vt = [get(), get()]
            for half in range(2):
                vb = banks[vset[half]]
                for ci in range(4):
                    c = half * 4 + ci
                    PE.op(lambda h, vb=vb, ci=ci, c=c: h.transpose(out=vb.ap[0:64, ci * 128:(ci + 1) * 128], in_=vc.ap[:, c * 64:(c + 1) * 64], identity=ident),
                          [vc, cst], [vb])
                DVE.op(lambda h, vb=vb, half=half, hd=hd: h.tensor_tensor(
                    out=vt[half].ap[0:64, 0:512].rearrange("p (c d) -> p c d", d=128), in0=vb.ap[0:64, :].rearrange("p (c d) -> p c d", d=128),
                    in1=cols.ap[:, 2, half * 4:(half + 1) * 4, hd].unsqueeze(2).broadcast_to([64, 4, 128]), op=ALU.mult), [vb, cols], [vt[half]])
            put(vc)
            yield
            kdt = get()
            kdtb = bf(kdt)
            kb6 = bbf(vset[2])
            for c in range(8):
                PE.op(lambda h, c=c: h.transpose(out=kb6.ap[0:64, c * 128:(c + 1) * 128], in_=qknb.ap[:, 512 + c * 64:512 + (c + 1) * 64], identity=identb.ap),
                      [qknb, identb], [kb6])
            DVE.op(lambda h, hd=hd: h.tensor_tensor(out=kdtb.ap[0:64, 0:1024].rearrange("p (c d) -> p c d", d=128),
                                                    in0=kb6.ap[0:64, :].rearrange("p (c d) -> p c d", d=128),
                                                    in1=cols.ap[:, 1, :, hd].unsqueeze(2).broadcast_to([64, 8, 128]), op=ALU.mult), [kb6, cols], [kdtb])
            yield
            kkb, qkb = banks[vset[0]], banks[vset[1]]
            for c in range(8):
                kc = qknb.ap[:, 512 + c * 64:512 + (c + 1) * 64]
                PE.op(lambda h, c=c, kc=kc: h.matmul(kkb.ap[0:64, c * 64:(c + 1) * 64], lhsT=kc, rhs=kc, start=True, stop=True), [qknb], [kkb])
            if main:
                for c in range(8):
                    kc = qknb.ap[:, 512 + c * 64:512 + (c + 1) * 64]
                    PE.op(lambda h, c=c, kc=kc: h.matmul(qkb.ap[0:64, c * 64:(c + 1) * 64], lhsT=kc, rhs=qknb.ap[:, c * 64:(c + 1) * 64],
                                                         start=True, stop=True), [qknb], [qkb])
            sv1, sv2 = get(), get()
            s1, s2 = bf(sv1), bf(sv2)
            Nn, NTt, IpNT, Pp = s1.ap[0:64, 0:512], s1.ap[0:64, 512:1024], s2.ap[0:64, 0:512], s2.ap[0:64, 512:1024]
            QKT = qdb.ap[0:64, 512:1024]
            id3 = cst.ap[0:64, C_ID:C_ID + 64].unsqueeze(1).broadcast_to([64, 8, 64])
            r3 = lambda ap_: ap_.rearrange("p (c j) -> p c j", j=64)
            DVE.op(lambda h: h.scalar_tensor_tensor(out=Nn, in0=kkb.ap[0:64, :], scalar=-1.0, in1=GT.ap[0:64, 0:TB], op0=ALU.mult, op1=ALU.mult),
                   [kkb, GT], [s1])
            if main:
                DVE.op(lambda h: h.scalar_tensor_tensor(out=QKT, in0=qkb.ap[0:64, :], scalar=-1.0, in1=DTc.ap[0:64, 0:TB], op0=ALU.mult, op1=ALU.mult), [qkb, DTc], [qdb])
            put(GT, DTc)
            yield
            POOL.op(lambda h: h.tensor_tensor(out=r3(Pp), in0=r3(Nn), in1=id3, op=ALU.add), [s1, cst], [s2])
            ntb = bbf(vset[2])
            for c in range(8):
                PE.op(lambda h, c=c: h.transpose(out=ntb.ap[0:64, c * 64:(c + 1) * 64], in_=Nn[:, c * 64:(c + 1) * 64], identity=identb.ap[0:64, 0:64]),
                      [s1, identb], [ntb])
            ACT.op(lambda h: h.activation(out=NTt, in_=ntb.ap[0:64, 0:512], func=AF.Copy), [ntb], [s1])
            yield
            pN, pNT, pP = banks[vset[0]], banks[vset[1]], banks[vset[2]]
            for lvl in range(1, 6):
                for c in range(8):
                    sl = slice(c * 64, (c + 1) * 64)
                    PE.op(lambda h, sl=sl: h.matmul(pNT.ap[0:64, sl], lhsT=Nn[:, sl], rhs=NTt[:, sl], start=True, stop=True), [s1], [pNT])
                if lvl < 5:
                    for c in range(8):
                        sl = slice(c * 64, (c + 1) * 64)
                        PE.op(lambda h, sl=sl: h.matmul(pN.ap[0:64, sl], lhsT=NTt[:, sl], rhs=Nn[:, sl], start=True, stop=True), [s1], [pN])
                DVE.op(lambda h: h.tensor_tensor(out=r3(IpNT), in0=r3(pNT.ap[0:64, :]), in1=id3, op=ALU.add), [pNT, cst], [s2])
                if lvl < 5:
                    ACT.op(lambda h: h.activation(out=NTt, in_=pNT.ap[0:64, :], func=AF.Copy), [pNT], [s1])
                    ACT.op(lambda h: h.activation(out=Nn, in_=pN.ap[0:64, :], func=AF.Copy), [pN], [s1])
                yield
                for c in range(8):
                    sl = slice(c * 64, (c + 1) * 64)
                    PE.op(lambda h, sl=sl: h.matmul(pP.ap[0:64, sl], lhsT=IpNT[:, sl], rhs=Pp[:, sl], start=True, stop=True), [s2], [pP])
                DVE.op(lambda h: h.tensor_copy(out=Pp, in_=pP.ap[0:64, :]), [pP], [s2])
                yield
            put(sv1)
            yield "SYNC"
            if prefetch is not None and hd == hgroups[-1][0]:
                prefetch()
            if os.environ.get('KDBG'):
                print('free slots at chunk loop', len(free), 'hd', hd, 'halfm', halfm)
            osb = get() if main else None
            for c in range(8):
                sl = slice(c * 64, (c + 1) * 64)
                kc = qknb.ap[:, 512 + c * 64:512 + (c + 1) * 64]
                i2 = hd
                PE.op(lambda h, kc=kc, hd=hd: h.matmul(sml.ap[0:64, 0:128], lhsT=kc, rhs=Sb[hd].ap, start=True, stop=True), [qknb, Sb[hd]], [sml])
                vsrc = vt[c // 4]
                DVE.op(lambda h, c=c, hd=hd, i2=i2, vsrc=vsrc: h.scalar_tensor_tensor(
                    out=Rb[i2].ap, in0=sml.ap[0:64, 0:128], scalar=cols.ap[:, 3, c, hd:hd + 1], in1=vsrc.ap[0:64, (c % 4) * 128:(c % 4 + 1) * 128],
                    op0=ALU.mult, op1=ALU.subtract), [sml, cols, vsrc], [Rb[i2]])
                yield
                PE.op(lambda h, sl=sl, i2=i2: h.matmul(sml.ap[0:64, 128:256], lhsT=Pp[:, sl], rhs=Rb[i2].ap, start=True, stop=True), [s2, Rb[i2]], [sml])
                ACT.op(lambda h, i2=i2: h.activation(out=vnb[i2].ap, in_=sml.ap[0:64, 128:256], func=AF.Copy), [sml], [vnb[i2]])
                yield
                if main:
                    PE.op(lambda h, sl=sl, hd=hd: h.matmul(sml.ap[:, 384:448], lhsT=Sb[hd].ap, rhs=qdb.ap[:, sl], start=True, stop=False), [Sb[hd], qdb], [sml])
                    PE.op(lambda h, sl=sl, i2=i2: h.matmul(sml.ap[:, 384:448], lhsT=vnb[i2].ap, rhs=QKT[:, sl], start=False, stop=True), [vnb[i2], qdb], [sml])
                    ACT.op(lambda h, sl=sl: h.activation(out=osb.ap[:, sl], in_=sml.ap[:, 384:448], func=AF.Copy), [sml], [osb])
                PE.op(lambda h, c=c, i2=i2: h.matmul(sml.ap[:, 256:384], lhsT=kdtb.ap[0:64, c * 128:(c + 1) * 128], rhs=vnb[i2].ap, start=True, stop=True),
                      [kdtb, vnb[i2]], [sml])
                DVE.op(lambda h, c=c, hd=hd: h.scalar_tensor_tensor(out=Sf[hd].ap, in0=Sf[hd].ap, scalar=gtot.ap[:, hd, c:c + 1], in1=sml.ap[:, 256:384],
                                                                  op0=ALU.mult, op1=ALU.subtract), [Sf[hd], gtot, sml], [Sf[hd]])
                ACT.op(lambda h, hd=hd: h.activation(out=Sb[hd].ap, in_=Sf[hd].ap, func=AF.Copy), [Sf[hd]], [Sb[hd]])
                yield
            put(vt[0], vt[1], kdt, sv2, qkn)
            if main:
                osq, rs = get(), get()
                ACT.op(lambda h: h.activation(out=osq.ap[:, 0:TB], in_=osb.ap[:, 0:TB], func=AF.Square), [osb], [osq])
                yield
                ssb = banks[ssbank]
                PE.op(lambda h: h.matmul(ssb.ap, lhsT=onesf, rhs=osq.ap[:, 0:TB], start=True, stop=True), [cst, osq], [ssb])
                if not halfm:
                    yield
                ACT.op(lambda h: h.activation(out=rs.ap[:, 0:TB], in_=ssb.ap, func=AF.Ln, scale=1.0 / 128, bias=EPS), [ssb], [rs])
                ACT.op(lambda h: h.activation(out=rs.ap[:, 0:TB], in_=rs.ap[:, 0:TB], func=AF.Exp, scale=-0.5), [rs], [rs])
                DVE.op(lambda h: h.scalar_tensor_tensor(out=osq.ap[:, 0:TB], in0=osb.ap[:, 0:TB], scalar=pcol(P_GNW), in1=rs.ap[:, 0:TB], op0=ALU.mult, op1=ALU.mult),
                       [osb, prm, rs], [osq])
                ps = hinproj((20 + hd) * 128)
                ACT.op(lambda h, ps=ps: h.activation(out=rs.ap[:, 0:TB], in_=ps.ap, func=AF.Silu), [ps], [rs])
                POOL.op(lambda h, hd=hd: h.tensor_tensor(out=yT[4 + hd].ap, in0=osq.ap[:, 0:TB], in1=rs.ap[:, 0:TB], op=ALU.mult), [osq, rs], [yT[4 + hd]])
                put(osq, rs, osb)
            put(qd)
            yield
        yield from rrobin([rg_task(), sc_task()])
        if halfm:
            yield from rrobin_sync([[head(i_) for i_ in grp] for grp in hgroups])
        else:
            for grp in hgroups:
                yield from rrobin([head(i_) for i_ in grp])
        put(G['brow'], G['gcrow'])
        if not main:
            return
        tokproj(yT, 8, T_WOUT, [2, 3, 6, 7] if halfm else [4, 5, 6, 7], resid_add)
        yield

    def zpart(blk, xt, xfull, hT):
        r0 = blk * TB
        resid_add = mk_resid(xt)
        POOL.dma(pb, T(pmain.buf, pmain.ap[r0:r0 + TB, :].rearrange("(t p) c -> p t c", p=128)), st_p)
        rmsnorm_to_hT(P_NFW, xt, xfull, hT)
        yield
        ffs = [get() for _ in range(NF // 2)]
        ffT = []
        for f in range(NF):
            sl_ = bf(ffs[f // 2])
            ffT.append(T(sl_.buf, sl_.ap[:, (f % 2) * TB:(f % 2 + 1) * TB]))
        for f in range(NF):
            tg, tu = ring_load(T_GU + 2 * f), ring_load(T_GU + 2 * f + 1)
            gb, ub_ = (banks[0], banks[1]) if f % 2 == 0 else (banks[4], banks[5])
            for (tl, bk) in ((tg, gb), (tu, ub_)):
                for k in range(8):
                    PE.op(lambda h, tl=tl, bk=bk, k=k: h.matmul(bk.ap, lhsT=tl.ap[:, k * 128:(k + 1) * 128], rhs=hT[k].ap, start=(k == 0), stop=(k == 7)),
                          [tl, hT[k]], [bk])
                    if k % 4 == 3:
                        yield
            sg = get()
            ACT.op(lambda h, gb=gb, sg=sg: h.activation(out=sg.ap[:, 0:TB], in_=gb.ap, func=AF.Tanh, scale=0.5), [gb], [sg])
            DVE.op(lambda h, gb=gb, sg=sg: h.scalar_tensor_tensor(out=sg.ap[:, 0:TB], in0=sg.ap[:, 0:TB], scalar=1.0, in1=gb.ap,
                                                                  op0=ALU.add, op1=ALU.mult), [gb, sg], [sg])
            DVE.op(lambda h, ub_=ub_, sg=sg, f=f: h.scalar_tensor_tensor(out=ffT[f].ap, in0=sg.ap[:, 0:TB], scalar=0.5, in1=ub_.ap,
                                                                         op0=ALU.mult, op1=ALU.mult), [ub_, sg], [ffT[f]])
            put(sg)
            yield
        yield from tokproj_g(ffT, NF, T_DOWN, [0, 1, 4, 5], resid_add)
        put(*ffs)
        yield
        rmsnorm_to_hT(P_NPW, xt, xfull, hT)
        ptb = bbf(0)
        for tt in range(4):
            for k in range(2):
                PE.op(lambda h, tt=tt, k=k: h.transpose(out=ptb.ap[:, k * 512 + tt * 128:k * 512 + (tt + 1) * 128], in_=pb.ap[:, tt, k * 128:(k + 1) * 128],
                                                        identity=identb.ap), [pb, identb], [ptb])
        for k in range(2):
            DVE.op(lambda h, k=k: h.tensor_copy(out=pT[k].ap, in_=ptb.ap[:, k * 512:(k + 1) * 512]), [ptb], [pT[k]])
        yield
        sig = {}

        def ep_gate(hf, tt, a):
            s_ = get()
            sig[(hf, tt)] = s_
            ACT.op(lambda h: h.activation(out=s_.ap[:, 0:TB], in_=a.ap, func=AF.Sigmoid), [a], [s_])


        def ep_proj(hf, tt, a):
            s_ = sig[(hf, tt)]
            x = xt[tt][hf]
            DVE.op(lambda h: h.tensor_tensor(out=s_.ap[:, 0:TB], in0=a.ap, in1=s_.ap[:, 0:TB], op=ALU.mult), [a, s_], [s_])
            POOL.op(lambda h: h.tensor_tensor(out=x.ap, in0=x.ap, in1=s_.ap[:, 0:TB], op=ALU.add), [x, s_], [x])
            put(s_)

        for hf_ in range(2):
            yield from tokproj_g(hT, 8, T_PG, [0, 1, 4, 5], ep_gate, bias=True, halves=(hf_,))
            yield from tokproj_g(pT, 2, T_PP, [0, 1, 4, 5], ep_proj, halves=(hf_,))
        for tt in range(4):
            ACT.op(lambda h, tt=tt: h.activation(out=junk.ap, in_=xfull[tt], func=AF.Square, accum_out=ssq.ap[:, tt:tt + 1]), xt[tt], [junk, ssq])
        ACT.op(lambda h: h.activation(out=ssq.ap[:, 4:8], in_=ssq.ap[:, 0:4], func=AF.Sqrt, scale=1.0 / D, bias=EPS), [ssq], [ssq])
        DVE.op(lambda h: h.reciprocal(out=ssq.ap[:, 4:8], in_=ssq.ap[:, 4:8]), [ssq], [ssq])
        for tt in range(4):
            o = ot[tt % 2]
            DVE.op(lambda h, tt=tt, o=o: h.scalar_tensor_tensor(out=o.ap, in0=xfull[tt], scalar=ssq.ap[:, 4 + tt:5 + tt], in1=rowb.ap,
                                                                op0=ALU.mult, op1=ALU.mult), xt[tt] + [ssq, rowb], [o])
            POOL.dma(T(out_d.buf, out_d.ap[r0 + tt * 128:r0 + (tt + 1) * 128, :]), o, st_ot[tt % 2])

    def mk_prefetch(xsrc, blk, par):
        return lambda: load_and_norm(xsrc, blk, xt_sets[par], xfull_sets[par], hT_sets[par], (0, 1, 4, 5))

    for b in range(npre):
        par = (b + npre) % 2
        if b + 1 < npre:
            pf = mk_prefetch(xpre, b + 1, 1 - par)
        else:
            pf = mk_prefetch(xmain, 0, 0)
        run_tasks([mixer(xpre, b, False, "one" if b == 0 else None, b == npre - 1, xt_sets[par], xfull_sets[par], hT_sets[par], [[0, 1, 2, 3]],
                         prenormed=(b > 0), prefetch=pf)])
    if npre > 0:
        for j in range(4):
            DVE.op(lambda h, j=j: h.tensor_scalar(out=hstate[j].ap, in0=hstate[j].ap, scalar1=pcol(P_FL), scalar2=None, op0=ALU.mult),
                   [hstate[j], prm], [hstate[j]])
    cv_issue(len(cv_jobs))
    fk0 = "flag" if npre > 0 else "one"
    run_tasks([mixer(xmain, 0, True, fk0, False, xt_sets[0], xfull_sets[0], hTA, [[0, 1, 2, 3]], prenormed=(npre > 0))])
    for b in range(nmain):
        tasks = [zpart(b, xt_sets[b % 2], xfull_sets[b % 2], hTB)]
        wts = [1]
        if b + 1 < nmain:
            tasks.append(mixer(xmain, b + 1, True, None, False, xt_sets[(b + 1) % 2], xfull_sets[(b + 1) % 2], hTA, [[0, 1], [2, 3]], halfm=True))
            wts = [int(os.environ.get('KWZ', '2')), int(os.environ.get('KWA', '1'))]
        run_tasks(tasks, wts)
    barrier()

    with nc.Block() as blk:
        @blk.sync
        def _(h):
            SP.replay(h)

        @blk.tensor
        def _(h):
            PE.replay(h)

        @blk.scalar
        def _(h):
            ACT.replay(h)

        @blk.vector
        def _(h):
            DVE.replay(h)

        @blk.gpsimd
        def _(h):
            POOL.replay(h)
    es.close()
    return nc


def make_consts():
    c = np.zeros((128, NCST), np.float32)
    c[:, C_ID:C_ID + 128] = np.eye(128, dtype=np.float32)
    c[:, C_ONE:C_ONE + 128] = 1.0
    j = np.arange(64)[:, None]
    i = np.arange(64)[None, :]
    c[0:64, C_CM:C_CM + 64] = (i >= j)
    c[0:64, C_SM:C_SM + 64] = (i > j)
    t = np.arange(512)
    c[0:4, C_RM:C_RM + 512] = (t % 64 != 0)[None, :]
    for h in range(4):
        c[h, C_SEL + h * 128:C_SEL + (h + 1) * 128] = 1.0
    return c


def make_params(inp, fl):
    p = np.zeros((128, NPRM), np.float32)
    col = lambda v: np.asarray(v, np.float32).reshape(-1, 128).T
    p[:, P_NMW:P_NMW + 8] = col(inp["norm_mix_w"][0])
    p[:, P_NFW:P_NFW + 8] = col(inp["norm_ffn_w"][0])
    p[:, P_NPW:P_NPW + 8] = col(inp["norm_ple_w"][0])
    caw = np.asarray(inp["conv_a_w"][0], np.float32)
    p[:, P_CAW:P_CAW + 16] = caw.reshape(4, 4, 128).transpose(2, 1, 0).reshape(128, 16)
    p[:, P_CAB:P_CAB + 4] = col(inp["conv_a_b"][0])
    p[:, P_BX:P_BX + 4] = col(inp["rg_bx"][0])
    p[:, P_BA:P_BA + 4] = col(inp["rg_ba"][0])
    p[:, P_LAM:P_LAM + 4] = col(inp["rg_lambda"][0])
    cqw = np.asarray(inp["conv_qkv_w"][0], np.float32)
    p[:, P_CQW:P_CQW + 48] = cqw.reshape(4, 12, 128).transpose(2, 1, 0).reshape(128, 48)
    p[:, P_GNW] = np.asarray(inp["gdn_norm_w"][0], np.float32)
    p[0:4, P_ALOG] = np.asarray(inp["gdn_a_log"][0], np.float32)
    p[0:4, P_DTB] = np.asarray(inp["gdn_dt_bias"][0], np.float32)
    p[:, P_FL] = fl
    p[:, P_FL + 1] = 1.0 - fl
    return p


def make_rgw(inp):
    r = np.zeros((128, 8, 128), np.float32)
    for gi, nm in enumerate(("rg_wx", "rg_wa")):
        w = np.asarray(inp[nm][0], np.float32)
        for j in range(4):
            for s in range(2):
                r[s * 64:(s + 1) * 64, gi * 4 + j, s * 64:(s + 1) * 64] = w[2 * j + s]
    return r.reshape(128, 8 * 128)


_NC_CACHE = {}


def run(inp, B, S, n_cores):
    half = S // 2
    nb = half // TB
    key = (nb, nb)
    if key not in _NC_CACHE:
        _NC_CACHE[key] = build(nb, nb)
    nc = _NC_CACHE[key]
    x = np.asarray(inp["x"], np.float32)
    p = np.asarray(inp["p"], np.float32)[0]
    cstv = make_consts()
    rgw = make_rgw(inp)
    shared = {
        "w_in": np.ascontiguousarray(np.asarray(inp["w_in"][0], np.float32)),
        "w_out": np.ascontiguousarray(np.asarray(inp["w_out"][0], np.float32)),
        "w_gate": np.ascontiguousarray(np.asarray(inp["w_gate"][0], np.float32)),
        "w_up": np.ascontiguousarray(np.asarray(inp["w_up"][0], np.float32)),
        "w_down": np.ascontiguousarray(np.asarray(inp["w_down"][0], np.float32)),
        "w_pg": np.ascontiguousarray(np.asarray(inp["w_ple_gate"][0], np.float32)),
        "w_pp": np.ascontiguousarray(np.asarray(inp["w_ple_proj"][0], np.float32)),
        "cst": cstv,
        "rowb": np.ascontiguousarray(np.broadcast_to(np.asarray(inp["norm_final_w"], np.float32)[None, :], (128, D))),
        "bpg": np.ascontiguousarray(np.asarray(inp["b_ple_gate"][0], np.float32)[None, :]),
        "rgw": rgw,
    }
    in_maps = []
    for core in range(n_cores):
        b, g = core // 2, core % 2
        m = dict(shared)
        m["xpre"] = np.ascontiguousarray(x[b, 0:half]) if g == 1 else np.zeros((half, D), np.float32)
        m["xmain"] = np.ascontiguousarray(x[b, g * half:(g + 1) * half])
        m["pmain"] = np.ascontiguousarray(p[b, g * half:(g + 1) * half])
        m["prm"] = make_params(inp, float(g))
        in_maps.append(m)
    res = run_bass_kernel_spmd(nc, in_maps, core_ids=list(range(n_cores)))
    out = np.zeros((B, S, D), np.float32)
    for core in range(n_cores):
        b, g = core // 2, core % 2
        out[b, g * half:(g + 1) * half] = res.results[core]["out"]
    return out


def kernel(**inputs):
    return run(inputs, 4, 8192, 8)
```
